# Optimizing a Trainium2 kernel written in Bass

```python
import math
import jax
import jax.numpy as jnp
from jax import lax
import numpy as np

D_MODEL = 2048
BATCH = 2
SEQ = 4096
DEPTH = 4
DEC_BATCH = 4
DEC_SEQ = 4096
PAST_LEN = 128

PLE_DIM = 256
GRID_W = 64
RMS_EPS = 1e-6

A_HEADS = 8
A_DK = 128
A_DV = 128
A_CONV_W = 5
DN_CHUNK = 64

B_HEADS = 8
B_DH = 128
NA_ROWS = 8
NA_COLS = 16

C_HEADS = 8
C_DH = 128
ROPE_THETA = 500000.0
ROPE_DIMS = C_DH // 4
Q_BLOCK = 128

D_FF = 8192
FFN_CONV_W = 3

N_AB = (DEPTH + 1) // 2
N_C = DEPTH // 2
A_QK_W = A_HEADS * A_DK
A_V_W = A_HEADS * A_DV
A_QKV_W = 2 * A_QK_W + A_V_W
B_W = B_HEADS * B_DH
AB_IN = A_QKV_W + A_V_W + 4 * A_HEADS + 3 * B_W
AB_MIX = A_V_W + B_W
C_QK_W = C_HEADS * 2 * C_DH
C_V_W = C_HEADS * 2 * C_DH
C_IN = 2 * C_QK_W + C_V_W

kernel_name = 'hybrid_deltanet_natten_diffattn_encoder'


def rmsnorm(x, w):
    xf = x.astype(jnp.float32)
    y = xf * lax.rsqrt(jnp.mean(xf * xf, axis=-1, keepdims=True) + RMS_EPS)
    return (y * w.astype(jnp.float32)).astype(x.dtype)


def l2norm(x):
    return x * lax.rsqrt(jnp.sum(x * x, axis=-1, keepdims=True) + 1e-6)


def dwconv_centred(x, w):
    k, s = w.shape[0], x.shape[1]
    xp = jnp.pad(x, ((0, 0), (k // 2, k // 2), (0, 0)))
    return sum(xp[:, i:i + s] * w[i].astype(x.dtype) for i in range(k))


def gated_delta_chunked(q, k, v, g, beta):
    bn, h, s, dk = k.shape
    dv = v.shape[-1]
    c = DN_CHUNK
    n = s // c
    q, k, v = (t.reshape(bn, h, n, c, t.shape[-1]) for t in (q, k, v))
    g = jnp.cumsum(g.reshape(bn, h, n, c), axis=-1)
    beta = beta.reshape(bn, h, n, c)
    incl = jnp.tril(jnp.ones((c, c), dtype=bool))
    strict = jnp.tril(jnp.ones((c, c), dtype=bool), -1)
    decay = jnp.exp(jnp.where(incl, g[..., :, None] - g[..., None, :], -jnp.inf))
    kb = k * beta[..., None]
    lower = jnp.where(strict, jnp.einsum('bhnid,bhnjd->bhnij', kb, k) * decay, 0.0)
    rhs = jnp.concatenate([v * beta[..., None], kb * jnp.exp(g)[..., None]], axis=-1)
    sol = lax.linalg.triangular_solve(lower + jnp.eye(c, dtype=lower.dtype), rhs,
                                      left_side=True, lower=True)
    u, w = sol[..., :dv], sol[..., dv:]
    intra = jnp.einsum('bhnid,bhnjd->bhnij', q, k) * decay

    def step(state, xs):
        q_c, k_c, u_c, w_c, g_c, a_c = xs
        v_new = u_c - jnp.einsum('bhck,bhkv->bhcv', w_c, state)
        o = (jnp.einsum('bhck,bhkv->bhcv', q_c * jnp.exp(g_c)[..., None], state)
             + jnp.einsum('bhcm,bhmv->bhcv', a_c, v_new))
        g_last = g_c[..., -1]
        state = (state * jnp.exp(g_last)[..., None, None]
                 + jnp.einsum('bhck,bhcv->bhkv', k_c * jnp.exp(g_last[..., None] - g_c)[..., None], v_new))
        return state, o

    xs = tuple(jnp.moveaxis(t, 2, 0) for t in (q, k, u, w, g, intra))
    _, o = lax.scan(step, jnp.zeros((bn, h, dk, dv), jnp.float32), xs)
    return jnp.moveaxis(o, 0, 2).reshape(bn, h, s, dv)


def deltanet_mixer(qkv, z, gates, conv_w, a_log, dt_bias, out_norm):
    bn, s, _ = qkv.shape
    f32 = jnp.float32
    qkv = jax.nn.silu(dwconv_centred(qkv, conv_w)).astype(f32)
    q = l2norm(qkv[..., :A_QK_W].reshape(bn, s, A_HEADS, A_DK)) * (A_DK ** -0.5)
    k = l2norm(qkv[..., A_QK_W:2 * A_QK_W].reshape(bn, s, A_HEADS, A_DK))
    v = qkv[..., 2 * A_QK_W:].reshape(bn, s, A_HEADS, A_DV)
    gates = gates.astype(f32).reshape(bn, s, 2, 2, A_HEADS)
    g = -jnp.exp(a_log.astype(f32)) * jax.nn.softplus(gates[:, :, 0] + dt_bias.astype(f32))
    beta = jax.nn.sigmoid(gates[:, :, 1])
    qh, kh, vh = (jnp.moveaxis(t, 1, 2) for t in (q, k, v))
    gh = jnp.moveaxis(g, 1, -1)
    bh = jnp.moveaxis(beta, 1, -1)
    o_fwd = gated_delta_chunked(qh, kh, vh, gh[:, 0], bh[:, 0])
    o_bwd = jnp.flip(gated_delta_chunked(jnp.flip(qh, 2), jnp.flip(kh, 2), jnp.flip(vh, 2),
                                         jnp.flip(gh[:, 1], -1), jnp.flip(bh[:, 1], -1)), 2)
    o = jnp.moveaxis(o_fwd + o_bwd, 1, 2)
    o = rmsnorm(o, out_norm) * jax.nn.silu(z.astype(f32).reshape(bn, s, A_HEADS, A_DV))
    return o.reshape(bn, s, A_V_W).astype(z.dtype)


def neighbourhood_attention(q, k, v, rpb):
    bn, s, h, d = q.shape
    rows = s // GRID_W
    wr = min(NA_ROWS, rows)
    r = np.arange(rows)
    row_start = np.clip(r - wr // 2, 0, rows - wr)
    row_idx = row_start[:, None] + np.arange(wr)[None, :]
    c = np.arange(GRID_W)
    col_start = np.clip(c - NA_COLS // 2, 0, GRID_W - NA_COLS)
    col_in = (c[None, :] >= col_start[:, None]) & (c[None, :] < col_start[:, None] + NA_COLS)
    dr = row_idx - r[:, None] + NA_ROWS - 1
    dc = np.clip(c[None, :] - c[:, None] + NA_COLS - 1, 0, 2 * NA_COLS - 2)
    bias = rpb[:, dr[:, None, :, None], dc[None, :, None, :]]
    bias = jnp.where(col_in[None, None, :, None, :], bias, -jnp.inf)
    qg = q.reshape(bn, rows, GRID_W, h, d)
    kg = k.reshape(bn, rows, GRID_W, h, d)[:, row_idx]
    vg = v.reshape(bn, rows, GRID_W, h, d)[:, row_idx]
    sc = jnp.einsum('brchd,brikhd->bhrcik', qg, kg, preferred_element_type=jnp.float32) * (d ** -0.5)
    p = jax.nn.softmax(sc + bias[None].astype(jnp.float32), axis=(-2, -1))
    o = jnp.einsum('bhrcik,brikhd->brchd', p.astype(v.dtype), vg)
    return o.reshape(bn, s, h * d)


def partial_rope(x, cos, sin):
    half = ROPE_DIMS // 2
    xf = x[..., :ROPE_DIMS].astype(jnp.float32)
    x1, x2 = xf[..., :half], xf[..., half:]
    cb = cos[None, :, None, None, :]
    sb = sin[None, :, None, None, :]
    rot = jnp.concatenate([x1 * cb - x2 * sb, x2 * cb + x1 * sb], axis=-1)
    return jnp.concatenate([rot.astype(x.dtype), x[..., ROPE_DIMS:]], axis=-1)


def rope_tables(s):
    inv = ROPE_THETA ** (-jnp.arange(0, ROPE_DIMS, 2, dtype=jnp.float32) / ROPE_DIMS)
    ang = jnp.arange(s, dtype=jnp.float32)[:, None] * inv[None, :]
    return jnp.cos(ang), jnp.sin(ang)


def mixer_ab(hn, w_in, conv_w, a_log, dt_bias, out_norm, rpb, w_out):
    bn, s, _ = hn.shape
    proj = hn @ w_in
    o1 = A_QKV_W
    o2 = o1 + A_V_W
    o3 = o2 + 4 * A_HEADS
    o_a = deltanet_mixer(proj[..., :o1], proj[..., o1:o2], proj[..., o2:o3],
                         conv_w, a_log, dt_bias, out_norm)
    qkv_b = proj[..., o3:].reshape(bn, s, 3, B_HEADS, B_DH)
    o_b = neighbourhood_attention(qkv_b[:, :, 0], qkv_b[:, :, 1], qkv_b[:, :, 2], rpb)
    return jnp.concatenate([o_a, o_b], axis=-1) @ w_out


def mixer_c(hn, w_in, lam_params, subln_w, w_out, cos, sin, layer):
    bn, s, _ = hn.shape
    proj = hn @ w_in
    q = proj[..., :C_QK_W].reshape(bn, s, C_HEADS, 2, C_DH)
    k = proj[..., C_QK_W:2 * C_QK_W].reshape(bn, s, C_HEADS, 2, C_DH)
    v = proj[..., 2 * C_QK_W:].reshape(bn, s, C_HEADS, 2 * C_DH)
    q = partial_rope(q, cos, sin)
    k = partial_rope(k, cos, sin)
    lambda_init = 0.8 - 0.6 * math.exp(-0.3 * layer)
    lp = lam_params.astype(jnp.float32)
    lam = jnp.exp(jnp.sum(lp[0] * lp[1])) - jnp.exp(jnp.sum(lp[2] * lp[3])) + lambda_init
    scale = C_DH ** -0.5
    nblk = s // Q_BLOCK
    qb = jnp.moveaxis(q.reshape(bn, nblk, Q_BLOCK, C_HEADS, 2, C_DH), 1, 0)

    def block(q_blk):
        sc = jnp.einsum('bqhjd,bkhjd->bhjqk', q_blk, k, preferred_element_type=jnp.float32) * scale
        pm = jax.nn.softmax(sc, axis=-1)
        a = pm[:, :, 0] - lam * pm[:, :, 1]
        return jnp.einsum('bhqk,bkhe->bqhe', a.astype(v.dtype), v)

    o = lax.map(block, qb)
    o = jnp.moveaxis(o, 0, 1).reshape(bn, s, C_HEADS, 2 * C_DH)
    o = rmsnorm(o, subln_w) * (1.0 - lambda_init)
    return o.reshape(bn, s, C_V_W) @ w_out


def conv_ffn(x, w_in, conv_w, conv_b, w_out):
    hid = dwconv_centred(x @ w_in, conv_w) + conv_b.astype(x.dtype)
    gate, up = hid[..., :D_FF], hid[..., D_FF:]
    return (jax.nn.gelu(gate, approximate=True) * up) @ w_out


def trunk(x, p, ab_w_in, ab_conv_w, ab_a_log, ab_dt_bias, ab_out_norm, ab_rpb, ab_w_out,
          c_w_in, c_lambda, c_subln, c_w_out, norms, ffn_w_in, ffn_conv_w, ffn_conv_b,
          ffn_w_out, ple_w_proj, ple_w_gate):
    cos, sin = rope_tables(x.shape[1])
    h = x
    for layer in range(DEPTH):
        j = layer // 2
        hn = rmsnorm(h, norms[layer, 0])
        if layer % 2 == 0:
            mix = mixer_ab(hn, ab_w_in[j], ab_conv_w[j], ab_a_log[j], ab_dt_bias[j],
                           ab_out_norm[j], ab_rpb[j], ab_w_out[j])
        else:
            mix = mixer_c(hn, c_w_in[j], c_lambda[j], c_subln[j], c_w_out[j], cos, sin, layer)
        h = h + rmsnorm(mix, norms[layer, 1])
        f = conv_ffn(rmsnorm(h, norms[layer, 2]), ffn_w_in[layer], ffn_conv_w[layer],
                     ffn_conv_b[layer], ffn_w_out[layer])
        h = h + rmsnorm(f, norms[layer, 3])
        h = h + jax.nn.sigmoid(h @ ple_w_gate[layer]) * (p[layer] @ ple_w_proj[layer])
    return h


def setup_inputs(seed: int = 0) -> dict:
    key = jax.random.key(seed)
    ks = jax.random.split(key, 24)
    f32 = jnp.float32

    def nrm(k, shape, scale):
        return jax.random.normal(k, shape, f32) * scale

    dt = jnp.exp(jax.random.uniform(ks[7], (N_AB, 2, A_HEADS), f32, math.log(1e-3), math.log(1e-1)))
    return {
        'x_prompt': nrm(ks[0], (BATCH, SEQ, D_MODEL), 1.0),
        'x_sample': nrm(ks[1], (DEC_BATCH, DEC_SEQ, D_MODEL), 1.0),
        'p_prompt': nrm(ks[2], (DEPTH, BATCH, SEQ, PLE_DIM), 1.0),
        'p_sample': nrm(ks[3], (DEPTH, DEC_BATCH, DEC_SEQ, PLE_DIM), 1.0),
        'ab_w_in': nrm(ks[4], (N_AB, D_MODEL, AB_IN), D_MODEL ** -0.5),
        'ab_conv_w': nrm(ks[5], (N_AB, A_CONV_W, A_QKV_W), A_CONV_W ** -0.5),
        'ab_a_log': jnp.log(jax.random.uniform(ks[6], (N_AB, 2, A_HEADS), f32, 1.0, 16.0)),
        'ab_dt_bias': dt + jnp.log(-jnp.expm1(-dt)),
        'ab_out_norm': 1.0 + nrm(ks[8], (N_AB, A_DV), 0.05),
        'ab_rpb': nrm(ks[9], (N_AB, B_HEADS, 2 * NA_ROWS - 1, 2 * NA_COLS - 1), 0.1),
        'ab_w_out': nrm(ks[10], (N_AB, AB_MIX, D_MODEL), AB_MIX ** -0.5),
        'c_w_in': nrm(ks[11], (N_C, D_MODEL, C_IN), D_MODEL ** -0.5),
        'c_lambda': nrm(ks[12], (N_C, 4, C_DH), 0.1),
        'c_subln': 1.0 + nrm(ks[13], (N_C, 2 * C_DH), 0.05),
        'c_w_out': nrm(ks[14], (N_C, C_V_W, D_MODEL), C_V_W ** -0.5),
        'norms': 1.0 + nrm(ks[15], (DEPTH, 4, D_MODEL), 0.05),
        'ffn_w_in': nrm(ks[16], (DEPTH, D_MODEL, 2 * D_FF), D_MODEL ** -0.5),
        'ffn_conv_w': nrm(ks[17], (DEPTH, FFN_CONV_W, 2 * D_FF), FFN_CONV_W ** -0.5),
        'ffn_conv_b': nrm(ks[18], (DEPTH, 2 * D_FF), 0.01),
        'ffn_w_out': nrm(ks[19], (DEPTH, D_FF, D_MODEL), D_FF ** -0.5),
        'ple_w_proj': nrm(ks[20], (DEPTH, PLE_DIM, D_MODEL), PLE_DIM ** -0.5),
        'ple_w_gate': nrm(ks[21], (DEPTH, D_MODEL, D_MODEL), D_MODEL ** -0.5),
    }


def reference(x_prompt, x_sample, p_prompt, p_sample, ab_w_in, ab_conv_w, ab_a_log, ab_dt_bias,
              ab_out_norm, ab_rpb, ab_w_out, c_w_in, c_lambda, c_subln, c_w_out, norms,
              ffn_w_in, ffn_conv_w, ffn_conv_b, ffn_w_out, ple_w_proj, ple_w_gate):
    y_prompt = trunk(x_prompt, p_prompt, ab_w_in, ab_conv_w, ab_a_log, ab_dt_bias, ab_out_norm,
                     ab_rpb, ab_w_out, c_w_in, c_lambda, c_subln, c_w_out, norms, ffn_w_in,
                     ffn_conv_w, ffn_conv_b, ffn_w_out, ple_w_proj, ple_w_gate)
    y_sample = trunk(x_sample, p_sample, ab_w_in, ab_conv_w, ab_a_log, ab_dt_bias, ab_out_norm,
                     ab_rpb, ab_w_out, c_w_in, c_lambda, c_subln, c_w_out, norms, ffn_w_in,
                     ffn_conv_w, ffn_conv_b, ffn_w_out, ple_w_proj, ple_w_gate)
    return (y_prompt, y_sample)
```

```python
import contextlib
import numpy as np
import concourse.bass as bass
import concourse.mybir as mybir
from concourse.bass_utils import run_bass_kernel_spmd

F32 = mybir.dt.float32
BF16 = mybir.dt.bfloat16
AF = mybir.ActivationFunctionType
ALU = mybir.AluOpType

SAME_ENGINE_SYNC = True
NDMA_SEM = 8


class Buf:
    __slots__ = ("w", "r", "excl")

    def __init__(self):
        self.w = None
        self.r = {}
        self.excl = False


class V:
    __slots__ = ("ap", "bufs")

    def __init__(self, ap, bufs):
        self.ap = ap
        self.bufs = bufs

    def __getitem__(self, idx):
        return V(self.ap[idx], self.bufs)

    def re(self, s, **kw):
        return V(self.ap.rearrange(s, **kw), self.bufs)

    def bc(self, shape):
        return V(self.ap.broadcast_to(shape), self.bufs)


class T:
    def __init__(self, ap, bufs=None):
        self.ap = ap
        self.bufs = bufs if bufs is not None else [Buf()]

    def __getitem__(self, idx):
        return V(self.ap[idx], self.bufs)

    def v(self, ap=None):
        return V(self.ap if ap is None else ap, self.bufs)

    def sub(self, idx):
        return T(self.ap[idx])


def VV(*vs):
    bufs = []
    for v in vs:
        bufs += v.bufs
    return V(vs[0].ap, bufs)


class Stream:
    def __init__(self, name):
        self.name = name
        self.prog = []
        self.sem = None
        self.count = 0
        self.known = {}
        self.dma_sems = []
        self.dma_count = 0


class Prog:
    def __init__(self, nc):
        self.nc = nc
        self.es = contextlib.ExitStack()
        self.st = {n: Stream(n) for n in ("pe", "act", "dve", "pool", "sp")}
        self.sems = {}
        for n, s in self.st.items():
            s.sem = "s_" + n
            self.sems[s.sem] = self.es.enter_context(nc.semaphore(s.sem))
        for n in ("sp", "pool", "act"):
            s = self.st[n]
            for i in range(NDMA_SEM):
                k = "d_%s%d" % (n, i)
                self.sems[k] = self.es.enter_context(nc.semaphore(k))
                s.dma_sems.append(k)
        self.uid = 0
        self.scopes = []
        self.nops = 0
        self.nwait = 0
        self.st["pe"].eng = nc.tensor
        self.st["act"].eng = nc.scalar
        self.st["dve"].eng = nc.vector
        self.st["pool"].eng = nc.gpsimd
        self.st["sp"].eng = nc.sync

    def sb(self, shape, dtype, name=None):
        self.uid += 1
        t = (self.scopes[-1] if self.scopes else self.es).enter_context(
            self.nc.sbuf_tensor("%s_%d" % (name or "sb", self.uid), list(shape), dtype))
        return T(t[tuple(slice(None) for _ in shape)])

    def ps(self, shape, dtype=F32, name=None):
        self.uid += 1
        t = (self.scopes[-1] if self.scopes else self.es).enter_context(
            self.nc.psum_tensor("%s_%d" % (name or "ps", self.uid), list(shape), dtype))
        r = T(t[tuple(slice(None) for _ in shape)])
        r.bufs[0].excl = True
        return r

    def dram(self, name, shape, dtype, kind="Internal"):
        return T(self.nc.dram_tensor(name, list(shape), dtype, kind=kind).ap())

    @contextlib.contextmanager
    def scope(self):
        self.barrier()
        es = contextlib.ExitStack()
        self.scopes.append(es)
        try:
            yield
        finally:
            self.barrier()
            self.scopes.pop()
            es.close()

    def emit(self, sn, fn, reads=(), writes=(), dma=False):
        S = self.st[sn]
        deps = {}

        def add(tok):
            if tok is None:
                return
            k, val = tok
            if deps.get(k, 0) < val:
                deps[k] = val

        for v in reads:
            for b in v.bufs:
                add(b.w)
                if b.excl:
                    for k, val in b.r.items():
                        if k != S.sem:
                            add((k, val))
        for v in writes:
            for b in v.bufs:
                add(b.w)
                for k, val in b.r.items():
                    add((k, val))
        if dma:
            n = S.dma_count
            S.dma_count += 1
            k = S.dma_sems[n % NDMA_SEM]
            if n >= NDMA_SEM:
                add((k, 16 * (n // NDMA_SEM)))
            tok = (k, 16 * (n // NDMA_SEM + 1))
            inc = 16
        else:
            S.count += 1
            tok = (S.sem, S.count)
            inc = 1
        e = S.eng
        for k, val in deps.items():
            if k == S.sem and (sn == "pe" or not SAME_ENGINE_SYNC):
                continue
            if S.known.get(k, 0) >= val:
                continue
            S.known[k] = val
            e.wait_ge(self.sems[k], val)
            self.nwait += 1
        fn(e).then_inc(self.sems[tok[0]], inc)
        self.nops += 1
        for v in reads:
            for b in v.bufs:
                if b.r.get(tok[0], 0) < tok[1]:
                    b.r[tok[0]] = tok[1]
        for v in writes:
            for b in v.bufs:
                b.w = tok
                b.r = {}
        return tok

    def barrier(self):
        toks = []
        for s in self.st.values():
            if s.count:
                toks.append((s.sem, s.count))
            for i, k in enumerate(s.dma_sems):
                n = s.dma_count
                cnt = (n - i + NDMA_SEM - 1) // NDMA_SEM if n > i else 0
                if cnt:
                    toks.append((k, 16 * cnt))
        for s in self.st.values():
            for k, val in toks:
                if k == s.sem:
                    continue
                if s.known.get(k, 0) >= val:
                    continue
                s.known[k] = val
                s.eng.wait_ge(self.sems[k], val)

    def finish(self):
        self.barrier()
        self.es.close()

    def mm(self, out, lhsT, rhs, start=True, stop=True):
        return self.emit("pe", lambda e: e.matmul(out.ap, lhsT.ap, rhs.ap, start=start, stop=stop),
                         reads=(lhsT, rhs), writes=(out,))

    def tr(self, out, in_, ident):
        return self.emit("pe", lambda e: e.transpose(out.ap, in_.ap, ident.ap),
                         reads=(in_, ident), writes=(out,))

    def act(self, out, in_, func, bias=None, scale=None, accum_out=None):
        kw = {}
        rd = [in_]
        if bias is not None:
            if isinstance(bias, V):
                kw["bias"] = bias.ap
                rd.append(bias)
            else:
                kw["bias"] = bias
        if scale is not None:
            if isinstance(scale, V):
                kw["scale"] = scale.ap
                rd.append(scale)
            else:
                kw["scale"] = scale
        wr = [out]
        if accum_out is not None:
            kw["accum_out"] = accum_out.ap
            wr.append(accum_out)
        return self.emit("act", lambda e: e.activation(out.ap, in_.ap, func, **kw), reads=rd, writes=wr)

    def _s(self, s, rd):
        if isinstance(s, V):
            rd.append(s)
            return s.ap
        return s

    def ts(self, out, in0, s1, op0, s2=None, op1=None, eng="dve"):
        rd = [in0]
        a1 = self._s(s1, rd)
        a2 = self._s(s2, rd)
        if op1 is None:
            return self.emit(eng, lambda e: e.tensor_scalar(out.ap, in0.ap, a1, a2, op0), reads=rd, writes=(out,))
        return self.emit(eng, lambda e: e.tensor_scalar(out.ap, in0.ap, a1, a2, op0, op1), reads=rd, writes=(out,))

    def tt(self, out, in0, in1, op, eng="dve"):
        return self.emit(eng, lambda e: e.tensor_tensor(out.ap, in0.ap, in1.ap, op), reads=(in0, in1), writes=(out,))

    def stt(self, out, in0, scalar, in1, op0, op1):
        rd = [in0, in1]
        a = self._s(scalar, rd)
        return self.emit("dve", lambda e: e.scalar_tensor_tensor(out.ap, in0.ap, a, in1.ap, op0, op1),
                         reads=rd, writes=(out,))

    def cp(self, out, in_, eng="dve"):
        if eng == "act":
            return self.emit("act", lambda e: e.copy(out.ap, in_.ap), reads=(in_,), writes=(out,))
        return self.emit(eng, lambda e: e.tensor_copy(out.ap, in_.ap), reads=(in_,), writes=(out,))

    def recip(self, out, in_):
        return self.emit("dve", lambda e: e.reciprocal(out.ap, in_.ap), reads=(in_,), writes=(out,))

    def memset(self, out, val, eng="pool"):
        return self.emit(eng, lambda e: e.memset(out.ap, val), writes=(out,))

    def dma(self, out, in_, q="sp", **kw):
        return self.emit(q, lambda e: e.dma_start(out=out.ap, in_=in_.ap, **kw), reads=(in_,), writes=(out,), dma=True)

    def red(self, out, in_, op, axis=None, eng="dve"):
        ax = mybir.AxisListType.X if axis is None else axis
        return self.emit(eng, lambda e: e.tensor_reduce(out.ap, in_.ap, ax, op), reads=(in_,), writes=(out,))

S = 4096
D = 2048
NT = S // 128
KC = D // 128
DEPTH = 4
PLE = 256
DFF = 8192
AB_IN = 7200
C_IN = 6144
EPS = 1e-6
NCORES = 6


class G:
    pass


def load_cols(P, g, src_rows, n, dst):
    with P.scope():
        st = P.sb([128, 128], F32, "lc")
        P.dma(st[0:n, :], src_rows)
        ps = g.bank[0]
        P.tr(ps[:, 0:n], st[0:n, :], g.ident_f[0:n, 0:n])
        P.cp(dst, ps[:, 0:n], eng="act")


def norm_stats(P, g, src, tn, rstd, sq, ps, dim=D):
    kcn = src.ap.shape[1]
    P.act(sq[:, 0:kcn, 0:tn], src, AF.Square)
    for kc in range(kcn):
        P.mm(ps[:, 0:tn], g.ones_b[:, :], sq[:, kc, 0:tn], start=(kc == 0), stop=(kc == kcn - 1))
    P.act(rstd, ps[:, 0:tn], AF.Sqrt, bias=g.epsc[:, 0:1], scale=1.0 / dim)
    P.recip(rstd, rstd)


def norm_apply(P, src, ncol, rstd, dst):
    kcn = src.ap.shape[1]
    for kc in range(kcn):
        P.stt(dst[:, kc, :], src[:, kc, :], ncol[:, kc:kc + 1], rstd, ALU.mult, ALU.mult)


def hblk(g, b):
    return g.hT_blk[b].v(g.hT_blk[b].ap.rearrange("(kc p) t -> p kc t", p=128))


def wsrc(w2d, c0, cn, k0=0, kn=None):
    kn = w2d.ap.shape[0] - k0 if kn is None else kn
    return V(w2d.ap[k0:k0 + kn, c0:c0 + cn].rearrange("(kc p) n -> p kc n", p=128), w2d.bufs)


def phase_in_transpose(P, g):
    with P.scope():
        xin = [P.sb([128, 4, D], F32, "xin") for _ in range(2)]
        hst = [P.sb([128, KC, 512], F32, "hst") for _ in range(2)]
        for b in range(8):
            xt = xin[b % 2]
            ht = hst[b % 2]
            P.dma(xt[:, :, :], V(g.x.ap[b * 512:(b + 1) * 512, :].rearrange("(j p) d -> p j d", p=128), g.x.bufs))
            for kc in range(KC):
                ps = g.bank[kc % 8]
                for j in range(4):
                    P.tr(ps[:, j * 128:(j + 1) * 128], xt[:, j, kc * 128:(kc + 1) * 128], g.ident_f[:, :])
                P.cp(ht[:, kc, :], ps[:, :], eng=("act" if kc % 2 else "dve"))
            P.dma(hblk(g, b), ht[:, :, :], q="pool")


def phase_out_transpose(P, g):
    with P.scope():
        hin = [P.sb([128, KC, 512], F32, "hin") for _ in range(2)]
        yst = [P.sb([128, 4, D], F32, "yst") for _ in range(2)]
        for b in range(8):
            ht = hin[b % 2]
            yt = yst[b % 2]
            P.dma(ht[:, :, :], hblk(g, b))
            i = 0
            for j in range(4):
                for k4 in range(4):
                    ps = g.bank[i % 8]
                    for kk in range(4):
                        kc = k4 * 4 + kk
                        P.tr(ps[:, kk * 128:(kk + 1) * 128], ht[:, kc, j * 128:(j + 1) * 128], g.ident_f[:, :])
                    P.cp(yt[:, j, k4 * 512:(k4 + 1) * 512], ps[:, :], eng=("act" if i % 2 else "dve"))
                    i += 1
            P.dma(V(g.y.ap[b * 512:(b + 1) * 512, :].rearrange("(j p) d -> p j d", p=128), g.y.bufs), yt[:, :, :], q="pool")


def phase_inproj(P, g, layer, ncol):
    j = layer // 2
    is_ab = (layer % 2 == 0)
    w = g.ab_w_in[j] if is_ab else g.c_w_in[j]
    TB = 1024
    with P.scope():
        hin = [P.sb([128, KC, 512], F32, "hin") for _ in range(1)]
        sq = P.sb([128, KC, 512], BF16, "sq")
        rstd = P.sb([128, 512], F32, "rstd")
        xT = [P.sb([128, KC, TB], BF16, "xT") for _ in range(2)]
        wb = [P.sb([128, KC, 512], BF16, "wb") for _ in range(2)]
        ost = [P.sb([128, 512], F32, "ost") for _ in range(3)]
        obf = [P.sb([128, 512], BF16, "obf") for _ in range(3)]
        vst = [P.sb([128, 4 * 129], BF16, "vst") for _ in range(2)]
        for v_ in vst:
            P.memset(v_[:, :], 1.0)
        if not is_ab:
            cost = P.sb([32, S], F32, "cost")
            sint = P.sb([32, S], F32, "sint")
            P.dma(cost[:, :], g.c_cos[:, :])
            P.dma(sint[:, :], g.c_sin[:, :])
            rt1 = [P.sb([32, 512], F32, "rt1") for _ in range(2)]
            rt2 = [P.sb([32, 512], F32, "rt2") for _ in range(2)]
        ps_stat = g.bank[0]
        obanks = [g.bank[1], g.bank[2], g.bank[3], g.bank[4]]
        rbanks = [g.bank[5], g.bank[6]]
        cnt = {"o": 0, "w": 0, "s": 0, "v": 0, "r": 0}

        def nextbank():
            b = obanks[cnt["o"] % 4]
            cnt["o"] += 1
            return b

        def loadw(c0, cn):
            t = wb[cnt["w"] % 2]
            cnt["w"] += 1
            P.dma(t[:, :, 0:cn], wsrc(w, c0, cn), q="pool")
            return t

        for tb in range(S // TB):
            x = xT[tb % 2]
            t0 = tb * TB
            for hb in range(2):
                hi = hin[0]
                P.dma(hi[:, :, :], hblk(g, tb * 2 + hb))
                norm_stats(P, g, hi[:, :, :], 512, rstd[:, :], sq, ps_stat)
                norm_apply(P, hi[:, :, :], ncol, rstd[:, :], x[:, :, hb * 512:(hb + 1) * 512])

            def fm_group(c0, epi):
                wt = loadw(c0, 512)
                for c in range(4):
                    for half in range(2):
                        ps = nextbank()
                        for kc in range(KC):
                            P.mm(ps[:, :], wt[:, kc, c * 128:(c + 1) * 128], x[:, kc, half * 512:(half + 1) * 512],
                                 start=(kc == 0), stop=(kc == KC - 1))
                        epi(ps, c, half)

            def tm_group(c0, cn, epi):
                wt = loadw(c0, cn)
                for tt in range(TB // 128):
                    ps = nextbank()
                    for kc in range(KC):
                        P.mm(ps[:, 0:cn], x[:, kc, tt * 128:(tt + 1) * 128], wt[:, kc, 0:cn],
                             start=(kc == 0), stop=(kc == KC - 1))
                    epi(ps, tt)

            def stage():
                i = cnt["s"] % 3
                cnt["s"] += 1
                return ost[i], obf[i]

            if is_ab:
                for grp in range(6):
                    def epi(ps, c, half, grp=grp):
                        of, _ = stage()
                        P.cp(of[:, :], ps[:, :], eng="act")
                        r0 = grp * 512 + c * 128
                        P.dma(g.qkvaT[r0:r0 + 128, 2 + t0 + half * 512: 2 + t0 + half * 512 + 512], of[:, :])
                    fm_group(grp * 512, epi)
                for grp in range(2):
                    def epi(ps, tt, grp=grp):
                        of, _ = stage()
                        P.cp(of[:, :], ps[:, :], eng="act")
                        P.dma(g.zbuf[t0 + tt * 128:t0 + tt * 128 + 128, grp * 512:(grp + 1) * 512], of[:, :])
                    tm_group(3072 + grp * 512, 512, epi)

                def epi(ps, tt):
                    of, _ = stage()
                    P.cp(of[:, 0:32], ps[:, 0:32], eng="act")
                    P.dma(g.gates[t0 + tt * 128:t0 + tt * 128 + 128, :], of[:, 0:32])
                tm_group(4096, 32, epi)
                for grp in range(4):
                    def epi(ps, c, half, grp=grp):
                        _, ob = stage()
                        P.cp(ob[:, :], ps[:, :], eng="act")
                        r0 = grp * 512 + c * 128
                        P.dma(g.qkbT[r0:r0 + 128, t0 + half * 512: t0 + half * 512 + 512], ob[:, :])
                    fm_group(4128 + grp * 512, epi)
                for grp in range(2):
                    def epi(ps, tt, grp=grp):
                        vs = vst[cnt["v"] % 2]
                        cnt["v"] += 1
                        P.cp(vs.v(vs.ap.rearrange("p (h e) -> p h e", e=129)[:, :, 0:128]),
                             ps.v(ps.ap.rearrange("p (h e) -> p h e", e=128)), eng="act")
                        P.dma(g.vb[t0 + tt * 128:t0 + tt * 128 + 128, grp * 516:(grp + 1) * 516], vs[:, :])
                    tm_group(6176 + grp * 512, 512, epi)
            else:
                import os
                DBG = os.environ.get("DBG", "")
                for grp in range(0 if "noqk" in DBG else 8):
                    def epi(ps, c, half, grp=grp):
                        _, ob = stage()
                        P.cp(ob[:, :], ps[:, :], eng="act")
                        if "norope" in DBG:
                            r0 = grp * 512 + c * 128
                            tok = slice(t0 + half * 512, t0 + half * 512 + 512)
                            P.dma(g.qkT[r0:r0 + 128, tok], ob[:, :])
                            return
                        rb = rbanks[cnt["r"] % 2]
                        a1 = rt1[cnt["r"] % 2]
                        a2 = rt2[cnt["r"] % 2]
                        cnt["r"] += 1
                        P.mm(rb[:, :], g.permT[:, :], ob[:, :])
                        tok = slice(t0 + half * 512, t0 + half * 512 + 512)
                        P.tt(a1[:, :], ps[0:32, :], cost[:, tok], ALU.mult)
                        P.tt(a2[:, :], rb[0:32, :], sint[:, tok], ALU.mult)
                        P.tt(ob[0:32, :], a1[:, :], a2[:, :], ALU.add, eng="dve")
                        r0 = grp * 512 + c * 128
                        P.dma(g.qkT[r0:r0 + 128, tok], ob[:, :])
                    fm_group(grp * 512, epi)
                for grp in range(0 if "nov" in DBG else 4):
                    def epi(ps, tt, grp=grp):
                        vs = vst[cnt["v"] % 2]
                        cnt["v"] += 1
                        P.cp(vs.v(vs.ap[:, 0:514].rearrange("p (h e) -> p h e", e=257)[:, :, 0:256]),
                             ps.v(ps.ap.rearrange("p (h e) -> p h e", e=256)), eng="act")
                        P.dma(g.vc[t0 + tt * 128:t0 + tt * 128 + 128, grp * 514:(grp + 1) * 514], vs[:, 0:514])
                    tm_group(4096 + grp * 512, 512, epi)


def phase_outproj(P, g, layer, ncol):
    j = layer // 2
    w = g.ab_w_out[j] if layer % 2 == 0 else g.c_w_out[j]
    with P.scope():
        mixb = [P.sb([128, KC, 512], BF16, "mixb") for _ in range(2)]
        mf = P.sb([128, KC, 512], F32, "mf")
        hin = [P.sb([128, KC, 512], F32, "hin") for _ in range(2)]
        sq = P.sb([128, KC, 512], BF16, "sq")
        rstd = P.sb([128, 512], F32, "rstd")
        wb = [P.sb([128, KC, 512], BF16, "wb") for _ in range(3)]
        wc = 0
        for b in range(8):
            mb = mixb[b % 2]
            hi = hin[b % 2]
            P.dma(mb[:, :, :], V(g.mixT.ap[:, b * 512:(b + 1) * 512].rearrange("(kc p) t -> p kc t", p=128), g.mix_blk[b].bufs))
            P.dma(hi[:, :, :], hblk(g, b))
            for grp in range(4):
                wt = wb[wc % 3]
                wc += 1
                P.dma(wt[:, :, :], wsrc(w, grp * 512, 512), q="pool")
                for c in range(4):
                    n = grp * 4 + c
                    ps = g.bank[1 + (n % 6)]
                    for kc in range(KC):
                        P.mm(ps[:, :], wt[:, kc, c * 128:(c + 1) * 128], mb[:, kc, :], start=(kc == 0), stop=(kc == KC - 1))
                    P.cp(mf[:, n, :], ps[:, :], eng="act")
            norm_stats(P, g, mf[:, :, :], 512, rstd[:, :], sq, g.bank[0])
            norm_apply(P, mf[:, :, :], ncol, rstd[:, :], mf[:, :, :])
            P.tt(hi[:, :, :], hi[:, :, :], mf[:, :, :], ALU.add, eng="pool")
            P.dma(hblk(g, b), hi[:, :, :], q="pool")


def phase_ffn1(P, g, layer, ncol):
    w = g.ffn_w_in[layer]
    TB = 1024
    with P.scope():
        cw = P.sb([128, 3, 128], F32, "cw")
        cb = P.sb([128, 128], F32, "cb")
        for i in range(3):
            load_cols(P, g, V(g.ffn_conv_w.ap[layer, i, :].rearrange("(c p) -> c p", p=128), g.ffn_conv_w.bufs), 128, cw[:, i, :])
        load_cols(P, g, V(g.ffn_conv_b.ap[layer, :].rearrange("(c p) -> c p", p=128), g.ffn_conv_b.bufs), 128, cb[:, :])
        hin = [P.sb([128, KC, 512], F32, "hin") for _ in range(1)]
        sq = P.sb([128, KC, 512], BF16, "sq")
        rstd = P.sb([128, 512], F32, "rstd")
        xT = [P.sb([128, KC, TB], BF16, "xT") for _ in range(2)]
        wbg = [P.sb([128, KC, 512], BF16, "wbg") for _ in range(2)]
        wbu = [P.sb([128, KC, 512], BF16, "wbu") for _ in range(2)]
        carry = P.sb([128, 128, 2], F32, "carry")
        P.memset(carry[:, :, :], 0.0)
        hbs = [P.sb([128, 516], F32, "hb") for _ in range(3)]
        accs = [P.sb([128, 512], F32, "acc") for _ in range(3)]
        gel = [P.sb([128, 512], F32, "gel") for _ in range(2)]
        aout = [P.sb([128, 512], BF16, "aout") for _ in range(3)]
        ps_stat = g.bank[0]
        pb = [g.bank[1 + i] for i in range(6)]
        cnt = {"p": 0, "h": 0, "a": 0}
        cst = [P.sb([128, 1024], BF16, "cst") for _ in range(2)]
        conv_tasks = [(g.ffn_w_out[layer], g.w2b, r, c_) for r in range(64) for c_ in range(2)] + \
                     [(g.ple_w_gate[layer], g.wgb, r, c_) for r in range(16) for c_ in range(2)]
        conv_i = [0]

        def do_conv(n):
            for _ in range(n):
                if conv_i[0] >= len(conv_tasks):
                    return
                src, dst, r, c_ = conv_tasks[conv_i[0]]
                st_ = cst[conv_i[0] % 2]
                conv_i[0] += 1
                P.dma(st_[:, :], src[r * 128:(r + 1) * 128, c_ * 1024:(c_ + 1) * 1024], q="pool")
                P.dma(dst[r * 128:(r + 1) * 128, c_ * 1024:(c_ + 1) * 1024], st_[:, :])

        def conv(ps, chunk, width, first):
            hb = hbs[cnt["h"] % 3]
            acc = accs[cnt["h"] % 3]
            cnt["h"] += 1
            P.cp(hb[:, 0:2], carry[:, chunk, :], eng="pool")
            if ps is not None:
                P.cp(hb[:, 2:2 + width], ps, eng="act")
                P.cp(carry[:, chunk, :], hb[:, width:width + 2], eng="pool")
            else:
                P.memset(hb[:, 2:2 + width], 0.0, eng="pool")
            P.act(acc[:, 0:width], hb[:, 0:width], AF.Identity, bias=cb[:, chunk:chunk + 1], scale=cw[:, 0, chunk:chunk + 1])
            P.stt(acc[:, 0:width], hb[:, 1:1 + width], cw[:, 1, chunk:chunk + 1], acc[:, 0:width], ALU.mult, ALU.add)
            P.stt(acc[:, 0:width], hb[:, 2:2 + width], cw[:, 2, chunk:chunk + 1], acc[:, 0:width], ALU.mult, ALU.add)
            return acc

        def gate_up(psg, psu, pair, width, tok0, skip_first):
            ag = conv(psg, pair, width, skip_first)
            au = conv(psu, 64 + pair, width, skip_first)
            ge = gel[cnt["a"] % 2]
            ao = aout[cnt["a"] % 3]
            cnt["a"] += 1
            P.act(ge[:, 0:width], ag[:, 0:width], AF.Gelu_apprx_tanh)
            P.tt(ao[:, 0:width], ge[:, 0:width], au[:, 0:width], ALU.mult, eng="pool")
            s = 1 if skip_first else 0
            if width - s == 1:
                P.dma(g.actT[pair * 128:(pair + 1) * 128, tok0 + s: tok0 + width], ao[:, s:width], allow_slow_non_contiguous=True)
            else:
                P.dma(g.actT[pair * 128:(pair + 1) * 128, tok0 + s: tok0 + width], ao[:, s:width])

        for tb in range(S // TB):
            x = xT[tb % 2]
            t0 = tb * TB
            for hb_ in range(2):
                hi = hin[0]
                P.dma(hi[:, :, :], hblk(g, tb * 2 + hb_))
                norm_stats(P, g, hi[:, :, :], 512, rstd[:, :], sq, ps_stat)
                norm_apply(P, hi[:, :, :], ncol, rstd[:, :], x[:, :, hb_ * 512:(hb_ + 1) * 512])
            for grp in range(16):
                idx = tb * 16 + grp
                wg = wbg[idx % 2]
                wu = wbu[idx % 2]
                if idx == 0:
                    P.dma(wg[:, :, :], wsrc(w, 0, 512), q="pool")
                    P.dma(wu[:, :, :], wsrc(w, DFF, 512), q="pool")
                do_conv(3)
                if idx + 1 < (S // TB) * 16:
                    g2 = (idx + 1) % 16
                    P.dma(wbg[(idx + 1) % 2][:, :, :], wsrc(w, g2 * 512, 512), q="pool")
                    P.dma(wbu[(idx + 1) % 2][:, :, :], wsrc(w, DFF + g2 * 512, 512), q="pool")
                for c in range(4):
                    pair = grp * 4 + c
                    for half in range(2):
                        psg = pb[(cnt["p"] * 2) % 6]
                        psu = pb[(cnt["p"] * 2 + 1) % 6]
                        cnt["p"] += 1
                        for kc in range(KC):
                            P.mm(psg[:, :], wg[:, kc, c * 128:(c + 1) * 128], x[:, kc, half * 512:(half + 1) * 512],
                                 start=(kc == 0), stop=(kc == KC - 1))
                        for kc in range(KC):
                            P.mm(psu[:, :], wu[:, kc, c * 128:(c + 1) * 128], x[:, kc, half * 512:(half + 1) * 512],
                                 start=(kc == 0), stop=(kc == KC - 1))
                        tok0 = t0 + half * 512 - 1
                        gate_up(psg[:, :], psu[:, :], pair, 512, tok0, skip_first=(tok0 < 0))
        do_conv(1000)
        for pair in range(64):
            gate_up(None, None, pair, 1, S - 1, False)


def phase_ffn2(P, g, layer, ncol):
    w = g.ffn_w_out[layer]
    with P.scope():
        actb = P.sb([128, 64, 512], BF16, "actb")
        actq = [actb.sub((slice(None), slice(q * 16, (q + 1) * 16), slice(None))) for q in range(4)]
        fT = P.sb([128, KC, 512], F32, "fT")
        hin = [P.sb([128, KC, 512], F32, "hin") for _ in range(2)]
        sq = P.sb([128, KC, 512], BF16, "sq")
        rstd = P.sb([128, 512], F32, "rstd")
        wk = [P.sb([128, 1024], BF16, "wk") for _ in range(4)]
        wc = 0
        for b in range(8):
            hi = hin[b % 2]
            P.dma(hi[:, :, :], hblk(g, b))
            for q in range(4):
                P.dma(actq[q][:, :, :], V(g.actT.ap[q * 2048:(q + 1) * 2048, b * 512:(b + 1) * 512].rearrange("(kc p) t -> p kc t", p=128), g.actT.bufs))
            for nh in range(2):
                for kc in range(64):
                    wt = wk[wc % 4]
                    wc += 1
                    P.dma(wt[:, :], g.w2b[kc * 128:(kc + 1) * 128, nh * 1024:(nh + 1) * 1024], q="pool")
                    for n in range(8):
                        P.mm(g.bank[n][:, :], wt[:, n * 128:(n + 1) * 128], actq[kc // 16][:, kc % 16, :],
                             start=(kc == 0), stop=(kc == 63))
                for n in range(8):
                    P.cp(fT[:, nh * 8 + n, :], g.bank[n][:, :], eng=("act" if n % 2 else "dve"))
            norm_stats(P, g, fT[:, :, :], 512, rstd[:, :], sq, g.bank[0])
            norm_apply(P, fT[:, :, :], ncol, rstd[:, :], fT[:, :, :])
            P.tt(hi[:, :, :], hi[:, :, :], fT[:, :, :], ALU.add, eng="pool")
            P.dma(hblk(g, b), hi[:, :, :], q="pool")


def phase_ple(P, g, layer):
    wg = g.ple_w_gate[layer]
    wp = g.ple_w_proj[layer]
    with P.scope():
        hin = [P.sb([128, KC, 512], F32, "hin") for _ in range(2)]
        hb16 = P.sb([128, KC, 512], BF16, "hb16")
        pin = [P.sb([128, 4, PLE], F32, "pin") for _ in range(2)]
        pT = P.sb([128, 2, 512], BF16, "pT")
        ppf = P.sb([128, 8, 512], F32, "ppf")
        sig = [P.sb([128, 512], F32, "sig") for _ in range(3)]
        wk = [P.sb([128, 1024], BF16, "wk") for _ in range(4)]
        wc = 0
        for b in range(8):
            hi = hin[b % 2]
            pi = pin[b % 2]
            P.dma(hi[:, :, :], hblk(g, b))
            P.dma(pi[:, :, :], V(g.p.ap[layer, b * 512:(b + 1) * 512, :].rearrange("(j p) d -> p j d", p=128), g.p.bufs))
            P.cp(hb16[:, :, :], hi[:, :, :], eng="pool")
            for k2 in range(2):
                ps = g.bank[k2]
                for jj in range(4):
                    P.tr(ps[:, jj * 128:(jj + 1) * 128], pi[:, jj, k2 * 128:(k2 + 1) * 128], g.ident_f[:, :])
                P.cp(pT[:, k2, :], ps[:, :], eng="act")
            for nh in range(2):
                for k2 in range(2):
                    wt = wk[wc % 4]
                    wc += 1
                    P.dma(wt[:, :], wp[k2 * 128:(k2 + 1) * 128, nh * 1024:(nh + 1) * 1024], q="pool")
                    for n in range(8):
                        P.mm(g.bank[n][:, :], wt[:, n * 128:(n + 1) * 128], pT[:, k2, :], start=(k2 == 0), stop=(k2 == 1))
                for n in range(8):
                    P.cp(ppf[:, n, :], g.bank[n][:, :], eng=("act" if n % 2 else "dve"))
                for kc in range(KC):
                    wt = wk[wc % 4]
                    wc += 1
                    P.dma(wt[:, :], g.wgb[kc * 128:(kc + 1) * 128, nh * 1024:(nh + 1) * 1024], q="pool")
                    for n in range(8):
                        P.mm(g.bank[n][:, :], wt[:, n * 128:(n + 1) * 128], hb16[:, kc, :], start=(kc == 0), stop=(kc == KC - 1))
                for n in range(8):
                    sg = sig[n % 3]
                    P.act(sg[:, :], g.bank[n][:, :], AF.Sigmoid)
                    P.tt(sg[:, :], sg[:, :], ppf[:, n, :], ALU.mult)
                    P.tt(hi[:, nh * 8 + n, :], hi[:, nh * 8 + n, :], sg[:, :], ALU.add, eng="pool")
            P.dma(hblk(g, b), hi[:, :, :], q="pool")

NEG = -30000.0


def na_classes():
    cls_of = {}
    tables = []
    for r0 in range(0, 64, 2):
        key = r0 if r0 in (0, 2, 60, 62) else -1
        if key in cls_of:
            continue
        cls_of[key] = len(tables)
        rb = min(max(r0 - 4, 0), 54)
        ent = []
        for jj in range(5):
            for rp in range(2):
                r = r0 + rp
                rs = min(max(r - 4, 0), 56)
                e = (rb - r0) + 8 + 2 * jj - rp
                val = [1 if rs <= rb + 2 * jj + i2 < rs + 8 else 0 for i2 in (0, 1)]
                ent.append((min(max(e, 0), 16), val[0], val[1]))
        tables.append(ent)
    return cls_of, tables


def na_host_consts():
    cls_of, tables = na_classes()
    c = np.arange(64)
    col_start = np.clip(c - 8, 0, 48)
    col_in = (c[None, :] >= col_start[:, None]) & (c[None, :] < col_start[:, None] + 16)
    maskT = np.where(col_in.T, 0.0, NEG).astype(np.float32)
    maskT = np.concatenate([maskT, maskT], 0)
    rowm = np.zeros((128, len(tables) * 10), np.float32)
    for ci, ent in enumerate(tables):
        for idx, (e, v0, v1) in enumerate(ent):
            rowm[:64, ci * 10 + idx] = 0.0 if v0 else NEG
            rowm[64:, ci * 10 + idx] = 0.0 if v1 else NEG
    return {"c_namask": maskT, "c_narow": rowm}


def rpb_pad(rpb):
    pad = np.full((2, 8, 18, 127), NEG, np.float32)
    pad[:, :, 1:16, 48:79] = rpb[:, :, :, ::-1]
    kc = np.arange(64)[:, None]
    c = np.arange(64)[None, :]
    pos = c - kc + 63
    out = np.empty((2, 8, 2, 64, 17, 64), np.float32)
    for i2 in range(2):
        sub = pad[:, :, i2:i2 + 17, :]
        out[:, :, i2] = np.transpose(sub[:, :, :, pos], (0, 1, 3, 2, 4))
    return np.ascontiguousarray(out.reshape(2, 8, 128, 17, 64))


def phase_na(P, g, layer):
    j = layer // 2
    scale = 128 ** -0.5
    cls_of, tables = na_classes()
    with P.scope():
        maskT = P.sb([128, 64], F32, "namask")
        rowm = P.sb([128, 50], F32, "narow")
        P.dma(maskT[:, :], g.c_namask[:, :])
        P.dma(rowm[:, :], g.c_narow[:, :])
        Wh = [P.sb([128, 17, 64], F32, "Wh") for _ in range(2)]
        bias = [P.sb([128, 5, 5, 128], F32, "nabias") for _ in range(2)]
        qTs = [P.sb([128, S], BF16, "nq") for _ in range(2)]
        kTs = [P.sb([128, S], BF16, "nk") for _ in range(2)]
        Vs = [P.sb([128, 32, 129], BF16, "nv") for _ in range(2)]
        sc = [P.sb([128, 640], F32, "nsc") for _ in range(2)]
        Eb = [P.sb([128, 5, 128], BF16, "nE") for _ in range(2)]
        ob = [P.sb([128, 128], F32, "nob") for _ in range(2)]
        rz = [P.sb([128, 1], F32, "nrz") for _ in range(2)]
        mst = [P.sb([128, 512], BF16, "nmst") for _ in range(2)]
        npair = 0
        for h in range(8):
            W = Wh[h % 2]
            bs = bias[h % 2]
            qT = qTs[h % 2]
            kT = kTs[h % 2]
            Vb = Vs[h % 2]
            P.dma(W[:, :, :], g.rpbpad[j, h, :, :, :])
            mb = bass.AP(tensor=maskT.ap.tensor, offset=maskT.ap.offset,
                         ap=[list(maskT.ap.ap[0]), [0, 17], [1, 64]])
            P.tt(W[:, :, :], W[:, :, :], V(mb, maskT.bufs), ALU.add, eng="pool")
            for ci, ent in enumerate(tables):
                for idx, (e, v0, v1) in enumerate(ent):
                    jj, rp = idx // 2, idx % 2
                    P.ts(bs[:, ci, jj, rp * 64:(rp + 1) * 64], W[:, e, :], rowm[:, ci * 10 + idx: ci * 10 + idx + 1], ALU.add,
                         eng="pool")
            P.dma(qT[:, :], g.qkbT[h * 128:(h + 1) * 128, :])
            P.dma(kT[:, :], g.qkbT[1024 + h * 128:1024 + (h + 1) * 128, :])
            P.dma(Vb[:, :, :], V(g.vb.ap[:, h * 129:(h + 1) * 129].rearrange("(kc p) e -> p kc e", p=128), g.vb.bufs))
            for r0 in range(0, 64, 2):
                ci = cls_of[r0 if r0 in (0, 2, 60, 62) else -1]
                rb = min(max(r0 - 4, 0), 54)
                pa = g.bank[(npair % 2) * 2]
                pb = g.bank[(npair % 2) * 2 + 1]
                po = g.bank[4 + npair % 2]
                tps = g.bank[6 + (npair // 4) % 2]
                s_ = sc[npair % 2]
                E = Eb[npair % 2]
                o_ = ob[npair % 2]
                z_ = rz[npair % 2]
                qs = slice(r0 * 64, r0 * 64 + 128)
                for jj in range(5):
                    kt0 = (rb + 2 * jj) * 64
                    dst = pa[:, jj * 128:(jj + 1) * 128] if jj < 4 else pb[:, 0:128]
                    P.mm(dst, kT[:, kt0:kt0 + 128], qT[:, qs])
                P.stt(s_[:, 0:512], pa[:, :], scale, bs.v(bs.ap[:, ci, 0:4, :].rearrange("p a b -> p (a b)")), ALU.mult, ALU.add)
                P.stt(s_[:, 512:640], pb[:, 0:128], scale, bs[:, ci, 4, :], ALU.mult, ALU.add)
                P.act(E.v(E.ap.rearrange("p a b -> p (a b)")), s_[:, :], AF.Exp)
                for jj in range(5):
                    P.mm(po[:, 0:129], E[:, jj, :], Vb[:, rb // 2 + jj, :], start=(jj == 0), stop=(jj == 4))
                P.recip(z_[:, :], po[:, 128:129])
                P.ts(o_[:, :], po[:, 0:128], z_[:, 0:1], ALU.mult)
                slot = npair % 4
                P.tr(tps[:, slot * 128:(slot + 1) * 128], o_[:, :], g.ident_f[:, :])
                if slot == 3:
                    ms = mst[(npair // 4) % 2]
                    P.cp(ms[:, :], tps[:, :], eng="act")
                    t0 = (r0 - 6) * 64
                    P.dma(g.mixT[1024 + h * 128:1024 + (h + 1) * 128, t0:t0 + 512], ms[:, :], q="pool")
                npair += 1


def phase_mixer_ab(P, g, layer):
    phase_delta(P, g, layer)
    phase_na(P, g, layer)

def delta_host_consts():
    m = np.arange(128)
    triF = (m[:, None] <= m[None, :]).astype(np.float32)
    triB = (m[:, None] >= m[None, :]).astype(np.float32)
    strF = (m[:, None] < m[None, :]).astype(np.float32)
    strB = (m[:, None] > m[None, :]).astype(np.float32)
    blk = [(m[:, None] // 16 == m[None, :] // 16).astype(np.float32)]
    for sz in (16, 32, 64):
        blk.append(((m[:, None] // (2 * sz) == m[None, :] // (2 * sz)) & (m[:, None] // sz != m[None, :] // sz)).astype(np.float32))
    return {"c_tri": np.stack([triF, triB], 0), "c_mstrict": np.stack([strF, strB], 0), "c_blkm": np.stack(blk, 0)}


def phase_delta(P, g, layer):
    j = layer // 2
    with P.scope():
        tri = P.sb([128, 2, 128], F32, "tri")
        mstr = P.sb([128, 2, 128], F32, "mstr")
        blkm = P.sb([128, 4, 128], F32, "blkm")
        for d in range(4):
            P.dma(blkm[:, d, :], g.c_blkm[d, :, :])
        ones_f = P.sb([128, 128], F32, "ones_f")
        onec = P.sb([128, 1], F32, "onec")
        zpad = P.sb([128, 2], F32, "zpad")
        onw = P.sb([128, 1, 128], F32, "onw")
        cwA = P.sb([128, 5, 24], F32, "cwA")
        for d in range(2):
            P.dma(tri[:, d, :], g.c_tri[d, :, :])
            P.dma(mstr[:, d, :], g.c_mstrict[d, :, :])
        P.memset(ones_f[:, :], 1.0)
        P.memset(onec[:, :], 1.0)
        P.memset(zpad[:, :], 0.0)
        P.dma(onw[:, :, :], V(g.ab_out_norm_full.ap[j:j + 1, :].partition_broadcast(128), g.ab_out_norm_full.bufs))
        for c in range(24):
            P.dma(g.qkvaT[c * 128:(c + 1) * 128, 0:2], zpad[:, :])
            P.dma(g.qkvaT[c * 128:(c + 1) * 128, S + 2:S + 4], zpad[:, :])
        for i in range(5):
            load_cols(P, g, V(g.ab_conv_w_full.ap[j, i, :].rearrange("(c p) -> c p", p=128), g.ab_conv_w_full.bufs), 24, cwA[:, i, :])

        def gt(name):
            return P.sb([128, 32, 16], F32, name)
        gdec, beta, negb, gc, egc, kdsc, eG, Gt = [gt(n) for n in ("gdec", "beta", "negb", "gc", "egc", "kdsc", "eG", "Gt")]
        with P.scope():
            Gm = P.sb([128, 32, 32], F32, "Gm")
            alb = P.sb([128, 1, 16], F32, "alb")
            dtb = P.sb([128, 1, 16], F32, "dtb")
            P.dma(Gm[:, :, :], V(g.gates.ap.rearrange("(t p) f -> p t f", p=128), g.gates.bufs))
            P.dma(alb[:, :, :], V(g.ab_a_log_full.ap[j:j + 1].rearrange("o a b -> o (a b)").partition_broadcast(128), g.ab_a_log_full.bufs))
            P.dma(dtb[:, :, :], V(g.ab_dt_bias_full.ap[j:j + 1].rearrange("o a b -> o (a b)").partition_broadcast(128), g.ab_dt_bias_full.bufs))

            def bc16(t):
                return V(bass.AP(tensor=t.ap.tensor, offset=t.ap.offset, ap=[list(t.ap.ap[0]), [0, 32], [1, 16]]), t.bufs)
            P.tt(gdec[:, :, :], Gm[:, :, 0:16], bc16(dtb), ALU.add)
            P.act(gdec[:, :, :], gdec[:, :, :], AF.Exp)
            P.act(gdec[:, :, :], gdec[:, :, :], AF.Ln, bias=onec[:, 0:1], scale=1.0)
            P.act(alb[:, :, :], alb[:, :, :], AF.Exp)
            P.ts(alb[:, :, :], alb[:, :, :], -1.0, ALU.mult)
            P.tt(gdec[:, :, :], gdec[:, :, :], bc16(alb), ALU.mult)
            P.act(beta[:, :, :], Gm[:, :, 16:32], AF.Sigmoid)
            P.ts(negb[:, :, :], beta[:, :, :], -1.0, ALU.mult)
            g2 = gdec.v(gdec.ap.rearrange("p t f -> p (t f)"))
            P.mm(g.bank[0][:, :], tri[:, 0, :], g2)
            P.mm(g.bank[1][:, :], tri[:, 1, :], g2)
            P.mm(g.bank[2][:, :], ones_f[:, :], g2)

            def b3(bk):
                return bk.v(bk.ap.rearrange("p (t f) -> p t f", f=16))
            P.cp(gc[:, :, 0:8], b3(g.bank[0])[:, :, 0:8], eng="act")
            P.cp(gc[:, :, 8:16], b3(g.bank[1])[:, :, 8:16], eng="act")
            P.cp(Gt[:, :, :], b3(g.bank[2]), eng="act")
            P.act(egc[:, :, :], gc[:, :, :], AF.Exp)
            P.act(eG[:, :, :], Gt[:, :, :], AF.Exp)
            P.tt(kdsc[:, :, :], Gt[:, :, :], gc[:, :, :], ALU.subtract)
            P.act(kdsc[:, :, :], kdsc[:, :, :], AF.Exp)

        qT = P.sb([128, S], BF16, "dqT")
        kT = P.sb([128, S], BF16, "dkT")
        ktok = P.sb([128, 32, 128], BF16, "ktok")
        vtok = P.sb([128, 32, 128], BF16, "vtok")
        oacc = P.sb([128, 32, 128], F32, "oacc")
        oacc_t = [oacc.sub((slice(None), t, slice(None))) for t in range(32)]
        oall = V(oacc.ap, [b for t in oacc_t for b in t.bufs])
        ost = [P.sb([128, 512], BF16, "dost") for _ in range(2)]
        ss32 = P.sb([128, 32], F32, "ss32")

        def f32t(n):
            return P.sb([128, 128], F32, n)

        def b16t(n):
            return P.sb([128, 128], BF16, n)
        gbc = [f32t("gbc") for _ in range(8)]
        kd = [b16t("kd") for _ in range(8)]
        kg = [b16t("kg") for _ in range(8)]
        tT = [f32t("tT") for _ in range(8)]
        Egc = [f32t("Egc") for _ in range(8)]
        DTs = [f32t("DTs") for _ in range(8)]
        DTi = [f32t("DTi") for _ in range(8)]
        Pb = [[f32t("Pb") for _ in range(2)] for _ in range(8)]
        Qb = [[f32t("Qb") for _ in range(2)] for _ in range(8)]
        Rb = [[f32t("Rb") for _ in range(2)] for _ in range(8)]
        Rt = [[f32t("Rt") for _ in range(2)] for _ in range(8)]
        NF = [f32t("NF") for _ in range(8)]
        QF = [f32t("QF") for _ in range(8)]
        NO = [f32t("NO") for _ in range(8)]
        NOT = [f32t("NOT") for _ in range(8)]
        Wb = [f32t("Wb") for _ in range(8)]
        Wpb = [f32t("Wpb") for _ in range(8)]
        TTb = [b16t("TTb") for _ in range(8)]
        tmpq = [f32t("tmpq") for _ in range(8)]
        intraT = [[b16t("intraT") for _ in range(8)] for _ in range(1)]
        QpT = [[f32t("QpT") for _ in range(8)] for _ in range(1)]
        McT = [[f32t("McT") for _ in range(8)] for _ in range(1)]
        Bc = [[f32t("Bc") for _ in range(8)] for _ in range(1)]
        uw = [[P.sb([128, 256], BF16, "uw") for _ in range(8)] for _ in range(1)]
        St = [[f32t("St") for _ in range(2)] for _ in range(2)]
        bk = g.bank

        def slot(b, ii):
            bb = b if ii < 4 else (b + 4) % 8
            return bk[bb][:, (ii % 4) * 128:(ii % 4 + 1) * 128]

        for h in range(8):
            nb = 0
            pre_scope = P.scope()
            pre_scope.__enter__()
            xin = [P.sb([128, 516], F32, "dxin") for _ in range(2)]
            cacc = [P.sb([128, 512], F32, "cacc") for _ in range(2)]
            ysil = [P.sb([128, 512], F32, "ysil") for _ in range(2)]
            sqb = P.sb([128, 512], BF16, "dsq")
            rn = P.sb([128, 512], F32, "drn")
            for blk in range(8):
                t0 = blk * 512
                for which in range(3):
                    c = which * 8 + h
                    xi = xin[nb % 2]
                    ac = cacc[nb % 2]
                    ys = ysil[nb % 2]
                    nb += 1
                    P.dma(xi[:, :], g.qkvaT[c * 128:(c + 1) * 128, t0:t0 + 516])
                    P.act(ac[:, :], xi[:, 0:512], AF.Identity, scale=cwA[:, 0, c:c + 1])
                    for i in range(1, 5):
                        P.stt(ac[:, :], xi[:, i:i + 512], cwA[:, i, c:c + 1], ac[:, :], ALU.mult, ALU.add)
                    P.act(ys[:, :], ac[:, :], AF.Silu)
                    if which < 2:
                        P.tt(sqb[:, :], ys[:, :], ys[:, :], ALU.mult, eng="pool")
                        P.mm(bk[0][:, :], g.ones_b[:, :], sqb[:, :])
                        P.act(rn[:, :], bk[0][:, :], AF.Sqrt, bias=g.epsc[:, 0:1], scale=1.0)
                        P.recip(rn[:, :], rn[:, :])
                    if which == 0:
                        P.stt(qT[:, t0:t0 + 512], ys[:, :], 128 ** -0.5, rn[:, :], ALU.mult, ALU.mult)
                    else:
                        if which == 1:
                            P.tt(ys[:, :], ys[:, :], rn[:, :], ALU.mult)
                            P.cp(kT[:, t0:t0 + 512], ys[:, :], eng="pool")
                        pb_ = bk[1 + which]
                        for jj in range(4):
                            P.tr(pb_[:, jj * 128:(jj + 1) * 128], ys[:, jj * 128:(jj + 1) * 128], g.ident_f[:, :])
                        dst = ktok if which == 1 else vtok
                        P.cp(dst.v(dst.ap[:, blk * 4:(blk + 1) * 4, :].rearrange("p a b -> p (a b)")), pb_[:, :], eng="act")
            pre_scope.__exit__(None, None, None)
            P.memset(oall, 0.0)
            for d in range(2):
                P.memset(St[d][0][:, :], 0.0)

            spar = [0, 0]
            for grp in range(8):
                par = 0
                insts = []
                for q_ in range(4):
                    insts.append((4 * grp + q_, 0))
                    insts.append((31 - 4 * grp - q_, 1))

                def col(X, ii):
                    t, d = insts[ii]
                    return X[:, t, d * 8 + h: d * 8 + h + 1]
                for ii, (t, d) in enumerate(insts):
                    P.ts(gbc[ii][:, :], ones_f[:, :], col(gdec, ii), ALU.mult, eng="pool")
                    P.ts(kd[ii][:, :], ktok[:, t, :], col(kdsc, ii), ALU.mult)
                    P.ts(kg[ii][:, :], ktok[:, t, :], col(egc, ii), ALU.mult)
                for ii, (t, d) in enumerate(insts):
                    ts_ = slice(t * 128, (t + 1) * 128)
                    P.mm(slot(0, ii), gbc[ii][:, :], tri[:, d, :])
                    P.mm(slot(1, ii), kT[:, ts_], kT[:, ts_])
                    P.mm(slot(2, ii), kT[:, ts_], qT[:, ts_])
                for ii, (t, d) in enumerate(insts):
                    P.ts(tT[ii][:, :], slot(0, ii), col(gc, ii), ALU.subtract, 0.0, ALU.min)
                    P.act(Egc[ii][:, :], slot(0, ii), AF.Exp)
                    P.act(tT[ii][:, :], tT[ii][:, :], AF.Exp)
                    P.tt(DTs[ii][:, :], tT[ii][:, :], mstr[:, d, :], ALU.mult, eng="pool")
                    P.tt(DTi[ii][:, :], tT[ii][:, :], tri[:, d, :], ALU.mult, eng="pool")
                    P.stt(NF[ii][:, :], slot(1, ii), col(negb, ii), DTs[ii][:, :], ALU.mult, ALU.mult)
                    P.tt(intraT[par][ii][:, :], slot(2, ii), DTi[ii][:, :], ALU.mult)
                    P.tt(Pb[ii][0][:, :], NF[ii][:, :], blkm[:, 0, :], ALU.mult, eng="pool")
                    P.tt(Rb[ii][0][:, :], Pb[ii][0][:, :], g.ident_f[:, :], ALU.add, eng="pool")
                for ii in range(8):
                    P.tr(slot(3, ii), NF[ii][:, :], g.ident_f[:, :])
                for ii in range(8):
                    P.cp(QF[ii][:, :], slot(3, ii), eng="act")
                    P.tt(Qb[ii][0][:, :], QF[ii][:, :], blkm[:, 0, :], ALU.mult, eng="pool")
                    P.tt(Rt[ii][0][:, :], Qb[ii][0][:, :], g.ident_f[:, :], ALU.add, eng="pool")
                for k in range(1, 5):
                    a, b_ = (k - 1) % 2, k % 2
                    for ii in range(8):
                        if k <= 2:
                            P.mm(slot(4, ii), Qb[ii][a][:, :], Pb[ii][a][:, :])
                        if k <= 3:
                            P.mm(slot(5, ii), Pb[ii][a][:, :], Qb[ii][a][:, :])
                        if k >= 2:
                            P.mm(slot(6, ii), Qb[ii][a][:, :], Rb[ii][b_][:, :])
                            P.mm(slot(3, ii), Rb[ii][b_][:, :], Qb[ii][a][:, :])
                    for ii in range(8):
                        if k >= 2:
                            P.tt(Rb[ii][a][:, :], slot(6, ii), Rb[ii][b_][:, :], ALU.add)
                            P.tt(Rt[ii][a][:, :], slot(3, ii), Rt[ii][b_][:, :], ALU.add)
                        if k <= 2:
                            P.cp(Pb[ii][b_][:, :], slot(4, ii), eng="act")
                        if k <= 3:
                            P.cp(Qb[ii][b_][:, :], slot(5, ii), eng="act")
                cur = 1
                for lv in range(3):
                    nxt = 1 - cur
                    for ii in range(8):
                        P.tt(NOT[ii][:, :], QF[ii][:, :], blkm[:, 1 + lv, :], ALU.mult, eng="pool")
                        if lv < 2:
                            P.tt(NO[ii][:, :], NF[ii][:, :], blkm[:, 1 + lv, :], ALU.mult, eng="pool")
                    for ii in range(8):
                        P.mm(slot(4, ii), NOT[ii][:, :], Rb[ii][cur][:, :])
                        if lv < 2:
                            P.mm(slot(5, ii), NO[ii][:, :], Rt[ii][cur][:, :])
                    for ii in range(8):
                        P.cp(Wb[ii][:, :], slot(4, ii), eng="act")
                        if lv < 2:
                            P.cp(Wpb[ii][:, :], slot(5, ii), eng="dve")
                    for ii in range(8):
                        P.mm(slot(6, ii), Rt[ii][cur][:, :], Wb[ii][:, :])
                        if lv < 2:
                            P.mm(slot(3, ii), Rb[ii][cur][:, :], Wpb[ii][:, :])
                    for ii in range(8):
                        if lv < 2:
                            P.tt(Rb[ii][nxt][:, :], slot(6, ii), Rb[ii][cur][:, :], ALU.add)
                            P.tt(Rt[ii][nxt][:, :], slot(3, ii), Rt[ii][cur][:, :], ALU.add)
                        else:
                            P.tt(TTb[ii][:, :], slot(6, ii), Rb[ii][cur][:, :], ALU.add)
                    cur = nxt
                for ii, (t, d) in enumerate(insts):
                    pbk = bk[ii // 2]
                    o_ = (ii % 2) * 256
                    P.mm(pbk[:, o_:o_ + 128], TTb[ii][:, :], vtok[:, t, :])
                    P.mm(pbk[:, o_ + 128:o_ + 256], TTb[ii][:, :], kg[ii][:, :])
                for ii, (t, d) in enumerate(insts):
                    pbk = bk[ii // 2]
                    o_ = (ii % 2) * 256
                    P.ts(uw[par][ii][:, :], pbk[:, o_:o_ + 256], col(beta, ii), ALU.mult)
                for ii, (t, d) in enumerate(insts):
                    P.mm(slot(2, ii), uw[par][ii][:, 128:256], intraT[par][ii][:, :])
                    P.mm(slot(3, ii), uw[par][ii][:, 128:256], kd[ii][:, :])
                    P.mm(slot(5, ii), kd[ii][:, :], uw[par][ii][:, 0:128])
                for ii, (t, d) in enumerate(insts):
                    ts_ = slice(t * 128, (t + 1) * 128)
                    P.tt(tmpq[ii][:, :], qT[:, ts_], Egc[ii][:, :], ALU.mult, eng="pool")
                    P.tt(QpT[par][ii][:, :], tmpq[ii][:, :], slot(2, ii), ALU.subtract)
                    P.stt(McT[par][ii][:, :], g.ident_f[:, :], col(eG, ii), slot(3, ii), ALU.mult, ALU.subtract)
                    P.cp(Bc[par][ii][:, :], slot(5, ii), eng="act")
                for ii, (t, d) in enumerate(insts):
                    so = St[d][spar[d] % 2]
                    sn = St[d][(spar[d] + 1) % 2]
                    spar[d] += 1
                    ps_s = bk[7][:, d * 128:(d + 1) * 128]
                    ps_o = bk[7][:, 256 + d * 128:256 + (d + 1) * 128]
                    P.mm(ps_s, McT[par][ii][:, :], so[:, :])
                    P.mm(ps_o, intraT[par][ii][:, :], uw[par][ii][:, 0:128], start=True, stop=False)
                    P.mm(ps_o, QpT[par][ii][:, :], so[:, :], start=False, stop=True)
                    P.tt(sn[:, :], ps_s, Bc[par][ii][:, :], ALU.add)
                    P.tt(oacc_t[t][:, :], ps_o, oacc_t[t][:, :], ALU.add)

            post_scope = P.scope()
            post_scope.__enter__()
            zt = P.sb([128, 32, 128], F32, "zt")
            sqbig = P.sb([128, 16, 128], F32, "sqbig")
            P.dma(zt[:, :, :], V(g.zbuf.ap[:, h * 128:(h + 1) * 128].rearrange("(t p) e -> p t e", p=128), g.zbuf.bufs))
            P.act(zt[:, :, :], zt[:, :, :], AF.Silu)
            for half in range(2):
                hs = slice(half * 16, (half + 1) * 16)
                tmp = V(oacc.ap[:, hs, :], oall.bufs)
                P.tt(sqbig[:, :, :], tmp, tmp, ALU.mult, eng="pool")
                P.red(ss32[:, hs], sqbig[:, :, :], ALU.add)
            P.act(ss32[:, :], ss32[:, :], AF.Sqrt, bias=g.epsc[:, 0:1], scale=1.0 / 128)
            P.recip(ss32[:, :], ss32[:, :])
            rb_ = V(bass.AP(tensor=ss32.ap.tensor, offset=ss32.ap.offset, ap=[list(ss32.ap.ap[0]), [1, 32], [0, 128]]), ss32.bufs)
            wb_ = V(bass.AP(tensor=onw.ap.tensor, offset=onw.ap.offset, ap=[list(onw.ap.ap[0]), [0, 32], [1, 128]]), onw.bufs)
            P.tt(oall, oall, rb_, ALU.mult)
            P.tt(oall, oall, wb_, ALU.mult, eng="pool")
            P.tt(oall, oall, zt[:, :, :], ALU.mult)
            for blk in range(8):
                pb_ = bk[blk % 2]
                for jj in range(4):
                    t = blk * 4 + jj
                    P.tr(pb_[:, jj * 128:(jj + 1) * 128], V(oacc.ap[:, t, :], oall.bufs), g.ident_f[:, :])
                os_ = ost[blk % 2]
                P.cp(os_[:, :], pb_[:, :], eng="act")
                P.dma(g.mixT[h * 128:(h + 1) * 128, blk * 512:(blk + 1) * 512], os_[:, :], q="pool")
            post_scope.__exit__(None, None, None)
import math


def phase_mixer_c(P, g, layer):
    j = layer // 2
    lam_init = 0.8 - 0.6 * math.exp(-0.3 * layer)
    scale = 128 ** -0.5
    with P.scope():
        lp = P.sb([128, 4, 128], F32, "lp")
        P.dma(lp[:, :, :], V(g.c_lambda_full.ap[j].partition_broadcast(128), g.c_lambda_full.bufs))
        pr = P.sb([128, 2, 128], F32, "pr")
        P.tt(pr[:, 0, :], lp[:, 0, :], lp[:, 1, :], ALU.mult)
        P.tt(pr[:, 1, :], lp[:, 2, :], lp[:, 3, :], ALU.mult)
        ssum = P.sb([128, 2], F32, "ssum")
        P.red(ssum[:, :], pr[:, :, :], ALU.add)
        P.act(ssum[:, :], ssum[:, :], AF.Exp)
        neglam = P.sb([128, 1], F32, "neglam")
        P.tt(neglam[:, :], ssum[:, 0:1], ssum[:, 1:2], ALU.subtract)
        P.ts(neglam[:, :], neglam[:, :], -1.0, ALU.mult, -lam_init, ALU.add)
        subw = P.sb([128, 256], F32, "subw")
        P.dma(subw[:, :], V(g.c_subln_full.ap[j:j + 1, :].partition_broadcast(128), g.c_subln_full.bufs))
        P.ts(subw[:, :], subw[:, :], 1.0 - lam_init, ALU.mult)

        qTs = [P.sb([128, 2, S], BF16, "qT") for _ in range(2)]
        kTs = [P.sb([128, 2, S], BF16, "kT") for _ in range(2)]
        Vhs = [P.sb([128, 32, 257], BF16, "Vh") for _ in range(2)]
        Es = [P.sb([128, 512], BF16, "E") for _ in range(3)]
        osb = [P.sb([128, 4, 257], F32, "osb") for _ in range(2)]
        rz = P.sb([128, 4], F32, "rz")
        abuf = [P.sb([128, 256], F32, "abuf") for _ in range(2)]
        junk = P.sb([128, 256], F32, "junk")
        mst = [P.sb([128, 2, 512], BF16, "mst") for _ in range(2)]
        obank = [g.bank[i] for i in range(4)]
        sbank = [g.bank[4], g.bank[5], g.bank[6]]
        tbank = g.bank[7]
        nqt = 0
        for h in range(8):
            qT = qTs[h % 2]
            kT = kTs[h % 2]
            Vh = Vhs[h % 2]
            for m in range(2):
                r0 = (h * 2 + m) * 128
                P.dma(qT[:, m, :], g.qkT[r0:r0 + 128, :])
                P.dma(kT[:, m, :], g.qkT[2048 + r0:2048 + r0 + 128, :])
            P.dma(Vh[:, :, :], V(g.vc.ap[:, h * 257:(h + 1) * 257].rearrange("(kc p) e -> p kc e", p=128), g.vc.bufs))
            for qb in range(8):
                qs = slice(qb * 512, (qb + 1) * 512)
                for m in range(2):
                    def smm(kc):
                        P.mm(sbank[kc % 3][:, :], kT[:, m, kc * 128:(kc + 1) * 128], qT[:, m, qs])
                    smm(0)
                    smm(1)
                    for kc in range(32):
                        if kc + 2 < 32:
                            smm(kc + 2)
                        E = Es[kc % 3]
                        P.act(E[:, :], sbank[kc % 3][:, :], AF.Exp, scale=scale)
                        for qt in range(4):
                            P.mm(obank[qt][:, 0:257], E[:, qt * 128:(qt + 1) * 128], Vh[:, kc, :], start=(kc == 0), stop=(kc == 31))
                    for qt in range(4):
                        P.cp(osb[m][:, qt, :], obank[qt][:, 0:257], eng="dve")
                ms = mst[qb % 2]
                for qt in range(4):
                    o1 = osb[0][:, qt, :]
                    o2 = osb[1][:, qt, :]
                    a = abuf[nqt % 2]
                    nqt += 1
                    P.recip(rz[:, 0:1], o1[:, 256:257])
                    P.recip(rz[:, 1:2], o2[:, 256:257])
                    P.tt(rz[:, 1:2], rz[:, 1:2], neglam[:, :], ALU.mult)
                    P.ts(a[:, :], o1[:, 0:256], rz[:, 0:1], ALU.mult)
                    P.stt(a[:, :], o2[:, 0:256], rz[:, 1:2], a[:, :], ALU.mult, ALU.add)
                    P.act(junk[:, :], a[:, :], AF.Square, accum_out=rz[:, 2:3])
                    P.act(rz[:, 3:4], rz[:, 2:3], AF.Sqrt, bias=g.epsc[:, 0:1], scale=1.0 / 256)
                    P.recip(rz[:, 3:4], rz[:, 3:4])
                    P.stt(a[:, :], a[:, :], rz[:, 3:4], subw[:, :], ALU.mult, ALU.mult)
                    for e2 in range(2):
                        P.tr(tbank[:, e2 * 128:(e2 + 1) * 128], a[:, e2 * 128:(e2 + 1) * 128], g.ident_f[:, :])
                    P.cp(ms[:, :, qt * 128:(qt + 1) * 128], tbank.v(tbank.ap[:, 0:256].rearrange("p (e q) -> p e q", e=2)), eng="act")
                P.dma(V(g.mixT.ap[h * 256:(h + 1) * 256, qs].rearrange("(e p) t -> p e t", p=128), g.mixT.bufs), ms[:, :, :], q="pool")

def host_consts():
    c = {}
    c["c_ident"] = np.eye(128, dtype=np.float32)
    inv = 500000.0 ** (-np.arange(0, 32, 2, dtype=np.float32) / 32.0)
    ang = np.arange(S, dtype=np.float32)[:, None] * inv[None, :].astype(np.float32)
    cos = np.cos(ang).astype(np.float32).T
    sin = np.sin(ang).astype(np.float32).T
    c["c_cos"] = np.ascontiguousarray(np.concatenate([cos, cos], 0))
    c["c_sin"] = np.ascontiguousarray(np.concatenate([sin, sin], 0))
    pm = np.zeros((128, 128), np.float32)
    for i in range(16):
        pm[i + 16, i] = -1.0
        pm[i, i + 16] = 1.0
    c["c_permT"] = pm
    c.update(delta_host_consts())
    c.update(na_host_consts())
    return c


WEIGHT_SPECS = [
    ("ab_w_in", [2, D, AB_IN]), ("ab_conv_w", [2, 5, 3072]), ("ab_a_log", [2, 2, 8]), ("ab_dt_bias", [2, 2, 8]),
    ("ab_out_norm", [2, 128]), ("ab_rpb", [2, 8, 15, 31]), ("ab_w_out", [2, D, D]),
    ("c_w_in", [2, D, C_IN]), ("c_lambda", [2, 4, 128]), ("c_subln", [2, 256]), ("c_w_out", [2, D, D]),
    ("norms", [4, 4, D]), ("ffn_w_in", [4, D, 2 * DFF]), ("ffn_conv_w", [4, 3, 2 * DFF]), ("ffn_conv_b", [4, 2 * DFF]),
    ("ffn_w_out", [4, DFF, D]), ("ple_w_proj", [4, PLE, D]), ("ple_w_gate", [4, D, D]),
]


def build_program(stages=None, debug=()):
    nc = bass.Bass("TRN2", target_bir_lowering=False)
    P = Prog(nc)
    g = G()
    g.x = P.dram("x", [S, D], F32, "ExternalInput")
    g.p = P.dram("p", [DEPTH, S, PLE], F32, "ExternalInput")
    g.y = P.dram("y", [S, D], F32, "ExternalOutput")
    for name, shape in WEIGHT_SPECS:
        t = P.dram(name, shape, F32, "ExternalInput")
        setattr(g, name + "_full", t)
    consts = host_consts()
    for name, arr in consts.items():
        setattr(g, name, P.dram(name, list(arr.shape), F32, "ExternalInput"))

    g.rpbpad = P.dram("rpbpad", [2, 8, 128, 17, 64], F32, "ExternalInput")

    def per(name, n):
        full = getattr(g, name + "_full")
        setattr(g, name, [T(full.ap[i], full.bufs) for i in range(n)])
    per("ab_w_in", 2); per("ab_w_out", 2); per("c_w_in", 2); per("c_w_out", 2)
    per("ffn_w_in", 4); per("ffn_w_out", 4); per("ple_w_proj", 4); per("ple_w_gate", 4)
    g.ffn_conv_w = g.ffn_conv_w_full
    g.ffn_conv_b = g.ffn_conv_b_full

    def scratch(name, shape, dt):
        return P.dram(name, shape, dt, "ExternalOutput" if name in debug else "Internal")
    g.hT = scratch("hT", [D, S], F32)
    g.hT_blk = [g.hT.sub((slice(None), slice(b * 512, (b + 1) * 512))) for b in range(8)]
    g.mixT = scratch("mixT", [D, S], BF16)
    g.mix_blk = [g.mixT.sub((slice(None), slice(b * 512, (b + 1) * 512))) for b in range(8)]
    g.actT = scratch("actT", [DFF, S], BF16)
    g.w2b = scratch("w2b", [DFF, D], BF16)
    g.wgb = scratch("wgb", [D, D], BF16)
    g.qkvaT = scratch("qkvaT", [3072, S + 4], F32)
    g.zbuf = scratch("zbuf", [S, 1024], F32)
    g.gates = scratch("gates", [S, 32], F32)
    g.qkbT = scratch("qkbT", [2048, S], BF16)
    g.vb = scratch("vb", [S, 8 * 129], BF16)
    g.qkT = scratch("qkT", [4096, S], BF16)
    g.vc = scratch("vc", [S, 8 * 257], BF16)

    g.ident_f = P.sb([128, 128], F32, "ident_f")
    g.ident_b = P.sb([128, 128], BF16, "ident_b")
    g.ones_b = P.sb([128, 128], BF16, "ones_b")
    g.epsc = P.sb([128, 1], F32, "epsc")
    g.permT = P.sb([128, 128], BF16, "permT")
    g.ncols = P.sb([128, 4, KC], F32, "ncols")
    g.bank = [P.ps([128, 512], F32, "bank%d" % i) for i in range(8)]
    P.dma(g.ident_f[:, :], g.c_ident[:, :])
    P.cp(g.ident_b[:, :], g.ident_f[:, :])
    P.memset(g.ones_b[:, :], 1.0)
    P.memset(g.epsc[:, :], EPS)
    with P.scope():
        pf = P.sb([128, 128], F32, "pf")
        P.dma(pf[:, :], g.c_permT[:, :])
        P.cp(g.permT[:, :], pf[:, :])

    st = stages
    def on(s):
        return st is None or s in st
    if on("in"):
        with nc.named_scope("in_tr"):
            phase_in_transpose(P, g)
    for layer in range(DEPTH):
        if not (st is None or ("L%d" % layer) in st):
            continue
        for i in range(4):
            load_cols(P, g, V(g.norms_full.ap[layer, i, :].rearrange("(c p) -> c p", p=128), g.norms_full.bufs), KC, g.ncols[:, i, :])
        if on("inproj"):
            with nc.named_scope("L%d_inproj" % layer):
                phase_inproj(P, g, layer, g.ncols[:, 0, :])
        if on("mixer"):
            if layer % 2 == 0:
                with nc.named_scope("L%d_delta" % layer):
                    phase_delta(P, g, layer)
                with nc.named_scope("L%d_na" % layer):
                    phase_na(P, g, layer)
            else:
                with nc.named_scope("L%d_mixc" % layer):
                    phase_mixer_c(P, g, layer)
        if on("outproj"):
            with nc.named_scope("L%d_outproj" % layer):
                phase_outproj(P, g, layer, g.ncols[:, 1, :])
        if st is not None and "dumph" in st:
            dm = P.dram("dbg_m%d" % layer, [D, S], F32, "ExternalOutput")
            P.barrier()
            for b in range(8):
                P.dma(dm[:, b * 512:(b + 1) * 512], g.hT_blk[b][:, :])
            P.barrier()
        if on("ffn1"):
            with nc.named_scope("L%d_ffn1" % layer):
                phase_ffn1(P, g, layer, g.ncols[:, 2, :])
        if on("ffn2"):
            with nc.named_scope("L%d_ffn2" % layer):
                phase_ffn2(P, g, layer, g.ncols[:, 3, :])
        if on("ple"):
            with nc.named_scope("L%d_ple" % layer):
                phase_ple(P, g, layer)
        if st is not None and "dumph" in st:
            dh = P.dram("dbg_h%d" % layer, [D, S], F32, "ExternalOutput")
            P.barrier()
            for b in range(8):
                P.dma(dh[:, b * 512:(b + 1) * 512], g.hT_blk[b][:, :])
            P.barrier()
    if on("out"):
        with nc.named_scope("out_tr"):
            phase_out_transpose(P, g)
    P.finish()
    return nc, P


_CACHE = {}


def make_in_maps(inputs, ncores=NCORES):
    consts = host_consts()
    xs = [inputs["x_prompt"][0], inputs["x_prompt"][1]] + [inputs["x_sample"][i] for i in range(4)]
    ps = [inputs["p_prompt"][:, 0], inputs["p_prompt"][:, 1]] + [inputs["p_sample"][:, i] for i in range(4)]
    maps = []
    rpbp = rpb_pad(np.asarray(inputs["ab_rpb"], dtype=np.float32))
    for c in range(ncores):
        s = c if c < 6 else c - 6
        m = {"x": np.ascontiguousarray(xs[s], dtype=np.float32), "p": np.ascontiguousarray(ps[s], dtype=np.float32)}
        for name, _ in WEIGHT_SPECS:
            m[name] = np.ascontiguousarray(inputs[name], dtype=np.float32)
        m.update(consts)
        m["rpbpad"] = rpbp
        maps.append(m)
    return maps


def kernel(**inputs):
    if "nc" not in _CACHE:
        _CACHE["nc"] = build_program()[0]
    nc = _CACHE["nc"]
    maps = make_in_maps(inputs)
    res = run_bass_kernel_spmd(nc, maps, core_ids=list(range(NCORES)))
    ys = [np.asarray(res.results[c]["y"], dtype=np.float32) for c in range(6)]
    y_prompt = np.stack(ys[0:2], 0)
    y_sample = np.stack(ys[2:6], 0)
    return (y_prompt, y_sample)
```

```python
import contextlib
import numpy as np
import concourse.bass as bass
import concourse.mybir as mybir
from concourse.bass_utils import run_bass_kernel_spmd

F32 = mybir.dt.float32
BF16 = mybir.dt.bfloat16
AF = mybir.ActivationFunctionType
ALU = mybir.AluOpType

SAME_ENGINE_SYNC = True
NDMA_SEM = 8


class Buf:
    __slots__ = ("w", "r", "excl")

    def __init__(self):
        self.w = None
        self.r = {}
        self.excl = False


class V:
    __slots__ = ("ap", "bufs")

    def __init__(self, ap, bufs):
        self.ap = ap
        self.bufs = bufs

    def __getitem__(self, idx):
        return V(self.ap[idx], self.bufs)

    def re(self, s, **kw):
        return V(self.ap.rearrange(s, **kw), self.bufs)

    def bc(self, shape):
        return V(self.ap.broadcast_to(shape), self.bufs)


class T:
    def __init__(self, ap, bufs=None):
        self.ap = ap
        self.bufs = bufs if bufs is not None else [Buf()]

    def __getitem__(self, idx):
        return V(self.ap[idx], self.bufs)

    def v(self, ap=None):
        return V(self.ap if ap is None else ap, self.bufs)

    def sub(self, idx):
        return T(self.ap[idx])


def VV(*vs):
    bufs = []
    for v in vs:
        bufs += v.bufs
    return V(vs[0].ap, bufs)


class Stream:
    def __init__(self, name):
        self.name = name
        self.prog = []
        self.sem = None
        self.count = 0
        self.known = {}
        self.dma_sems = []
        self.dma_count = 0


class Prog:
    def __init__(self, nc):
        self.nc = nc
        self.es = contextlib.ExitStack()
        self.st = {n: Stream(n) for n in ("pe", "act", "dve", "pool", "sp")}
        self.sems = {}
        for n, s in self.st.items():
            s.sem = "s_" + n
            self.sems[s.sem] = self.es.enter_context(nc.semaphore(s.sem))
        for n in ("sp", "pool", "act"):
            s = self.st[n]
            for i in range(NDMA_SEM):
                k = "d_%s%d" % (n, i)
                self.sems[k] = self.es.enter_context(nc.semaphore(k))
                s.dma_sems.append(k)
        self.uid = 0
        self.scopes = []
        self.nops = 0
        self.nwait = 0
        self.st["pe"].eng = nc.tensor
        self.st["act"].eng = nc.scalar
        self.st["dve"].eng = nc.vector
        self.st["pool"].eng = nc.gpsimd
        self.st["sp"].eng = nc.sync

    def sb(self, shape, dtype, name=None):
        self.uid += 1
        t = (self.scopes[-1] if self.scopes else self.es).enter_context(
            self.nc.sbuf_tensor("%s_%d" % (name or "sb", self.uid), list(shape), dtype))
        return T(t[tuple(slice(None) for _ in shape)])

    def ps(self, shape, dtype=F32, name=None):
        self.uid += 1
        t = (self.scopes[-1] if self.scopes else self.es).enter_context(
            self.nc.psum_tensor("%s_%d" % (name or "ps", self.uid), list(shape), dtype))
        r = T(t[tuple(slice(None) for _ in shape)])
        r.bufs[0].excl = True
        return r

    def dram(self, name, shape, dtype, kind="Internal"):
        return T(self.nc.dram_tensor(name, list(shape), dtype, kind=kind).ap())

    @contextlib.contextmanager
    def scope(self):
        self.barrier()
        es = contextlib.ExitStack()
        self.scopes.append(es)
        try:
            yield
        finally:
            self.barrier()
            self.scopes.pop()
            es.close()

    def emit(self, sn, fn, reads=(), writes=(), dma=False):
        S = self.st[sn]
        deps = {}

        def add(tok):
            if tok is None:
                return
            k, val = tok
            if deps.get(k, 0) < val:
                deps[k] = val

        for v in reads:
            for b in v.bufs:
                add(b.w)
                if b.excl:
                    for k, val in b.r.items():
                        if k != S.sem:
                            add((k, val))
        for v in writes:
            for b in v.bufs:
                add(b.w)
                for k, val in b.r.items():
                    add((k, val))
        if dma:
            n = S.dma_count
            S.dma_count += 1
            k = S.dma_sems[n % NDMA_SEM]
            if n >= NDMA_SEM:
                add((k, 16 * (n // NDMA_SEM)))
            tok = (k, 16 * (n // NDMA_SEM + 1))
            inc = 16
        else:
            S.count += 1
            tok = (S.sem, S.count)
            inc = 1
        e = S.eng
        for k, val in deps.items():
            if k == S.sem and (sn == "pe" or not SAME_ENGINE_SYNC):
                continue
            if S.known.get(k, 0) >= val:
                continue
            S.known[k] = val
            e.wait_ge(self.sems[k], val)
            self.nwait += 1
        fn(e).then_inc(self.sems[tok[0]], inc)
        self.nops += 1
        for v in reads:
            for b in v.bufs:
                if b.r.get(tok[0], 0) < tok[1]:
                    b.r[tok[0]] = tok[1]
        for v in writes:
            for b in v.bufs:
                b.w = tok
                b.r = {}
        return tok

    def barrier(self):
        toks = []
        for s in self.st.values():
            if s.count:
                toks.append((s.sem, s.count))
            for i, k in enumerate(s.dma_sems):
                n = s.dma_count
                cnt = (n - i + NDMA_SEM - 1) // NDMA_SEM if n > i else 0
                if cnt:
                    toks.append((k, 16 * cnt))
        for s in self.st.values():
            for k, val in toks:
                if k == s.sem:
                    continue
                if s.known.get(k, 0) >= val:
                    continue
                s.known[k] = val
                s.eng.wait_ge(self.sems[k], val)

    def finish(self):
        self.barrier()
        self.es.close()

    def mm(self, out, lhsT, rhs, start=True, stop=True):
        return self.emit("pe", lambda e: e.matmul(out.ap, lhsT.ap, rhs.ap, start=start, stop=stop),
                         reads=(lhsT, rhs), writes=(out,))

    def tr(self, out, in_, ident):
        return self.emit("pe", lambda e: e.transpose(out.ap, in_.ap, ident.ap),
                         reads=(in_, ident), writes=(out,))

    def act(self, out, in_, func, bias=None, scale=None, accum_out=None):
        kw = {}
        rd = [in_]
        if bias is not None:
            if isinstance(bias, V):
                kw["bias"] = bias.ap
                rd.append(bias)
            else:
                kw["bias"] = bias
        if scale is not None:
            if isinstance(scale, V):
                kw["scale"] = scale.ap
                rd.append(scale)
            else:
                kw["scale"] = scale
        wr = [out]
        if accum_out is not None:
            kw["accum_out"] = accum_out.ap
            wr.append(accum_out)
        return self.emit("act", lambda e: e.activation(out.ap, in_.ap, func, **kw), reads=rd, writes=wr)

    def _s(self, s, rd):
        if isinstance(s, V):
            rd.append(s)
            return s.ap
        return s

    def ts(self, out, in0, s1, op0, s2=None, op1=None, eng="dve"):
        rd = [in0]
        a1 = self._s(s1, rd)
        a2 = self._s(s2, rd)
        if op1 is None:
            return self.emit(eng, lambda e: e.tensor_scalar(out.ap, in0.ap, a1, a2, op0), reads=rd, writes=(out,))
        return self.emit(eng, lambda e: e.tensor_scalar(out.ap, in0.ap, a1, a2, op0, op1), reads=rd, writes=(out,))

    def tt(self, out, in0, in1, op, eng="dve"):
        return self.emit(eng, lambda e: e.tensor_tensor(out.ap, in0.ap, in1.ap, op), reads=(in0, in1), writes=(out,))

    def stt(self, out, in0, scalar, in1, op0, op1):
        rd = [in0, in1]
        a = self._s(scalar, rd)
        return self.emit("dve", lambda e: e.scalar_tensor_tensor(out.ap, in0.ap, a, in1.ap, op0, op1),
                         reads=rd, writes=(out,))

    def cp(self, out, in_, eng="dve"):
        if eng == "act":
            return self.emit("act", lambda e: e.copy(out.ap, in_.ap), reads=(in_,), writes=(out,))
        return self.emit(eng, lambda e: e.tensor_copy(out.ap, in_.ap), reads=(in_,), writes=(out,))

    def recip(self, out, in_):
        return self.emit("dve", lambda e: e.reciprocal(out.ap, in_.ap), reads=(in_,), writes=(out,))

    def memset(self, out, val, eng="pool"):
        return self.emit(eng, lambda e: e.memset(out.ap, val), writes=(out,))

    def dma(self, out, in_, q="sp", **kw):
        return self.emit(q, lambda e: e.dma_start(out=out.ap, in_=in_.ap, **kw), reads=(in_,), writes=(out,), dma=True)

    def red(self, out, in_, op, axis=None, eng="dve"):
        ax = mybir.AxisListType.X if axis is None else axis
        return self.emit(eng, lambda e: e.tensor_reduce(out.ap, in_.ap, ax, op), reads=(in_,), writes=(out,))

S = 4096
D = 2048
NT = S // 128
KC = D // 128
DEPTH = 4
PLE = 256
DFF = 8192
AB_IN = 7200
C_IN = 6144
EPS = 1e-6
NCORES = 6


class G:
    pass


def load_cols(P, g, src_rows, n, dst):
    with P.scope():
        st = P.sb([128, 128], F32, "lc")
        P.dma(st[0:n, :], src_rows)
        ps = g.bank[0]
        P.tr(ps[:, 0:n], st[0:n, :], g.ident_f[0:n, 0:n])
        P.cp(dst, ps[:, 0:n], eng="act")


def norm_stats(P, g, src, tn, rstd, sq, ps, dim=D):
    kcn = src.ap.shape[1]
    P.act(sq[:, 0:kcn, 0:tn], src, AF.Square)
    for kc in range(kcn):
        P.mm(ps[:, 0:tn], g.ones_b[:, :], sq[:, kc, 0:tn], start=(kc == 0), stop=(kc == kcn - 1))
    P.act(rstd, ps[:, 0:tn], AF.Sqrt, bias=g.epsc[:, 0:1], scale=1.0 / dim)
    P.recip(rstd, rstd)


def norm_apply(P, src, ncol, rstd, dst):
    kcn = src.ap.shape[1]
    for kc in range(kcn):
        P.stt(dst[:, kc, :], src[:, kc, :], ncol[:, kc:kc + 1], rstd, ALU.mult, ALU.mult)


def hblk(g, b):
    return g.hT_blk[b].v(g.hT_blk[b].ap.rearrange("(kc p) t -> p kc t", p=128))


def wsrc(w2d, c0, cn, k0=0, kn=None):
    kn = w2d.ap.shape[0] - k0 if kn is None else kn
    return V(w2d.ap[k0:k0 + kn, c0:c0 + cn].rearrange("(kc p) n -> p kc n", p=128), w2d.bufs)


def phase_in_transpose(P, g):
    with P.scope():
        xin = [P.sb([128, 4, D], F32, "xin") for _ in range(2)]
        hst = [P.sb([128, KC, 512], F32, "hst") for _ in range(2)]
        for b in range(8):
            xt = xin[b % 2]
            ht = hst[b % 2]
            P.dma(xt[:, :, :], V(g.x.ap[b * 512:(b + 1) * 512, :].rearrange("(j p) d -> p j d", p=128), g.x.bufs))
            for kc in range(KC):
                ps = g.bank[kc % 8]
                for j in range(4):
                    P.tr(ps[:, j * 128:(j + 1) * 128], xt[:, j, kc * 128:(kc + 1) * 128], g.ident_f[:, :])
                P.cp(ht[:, kc, :], ps[:, :], eng=("act" if kc % 2 else "dve"))
            P.dma(hblk(g, b), ht[:, :, :], q="pool")


def phase_out_transpose(P, g):
    with P.scope():
        hin = [P.sb([128, KC, 512], F32, "hin") for _ in range(2)]
        yst = [P.sb([128, 4, D], F32, "yst") for _ in range(2)]
        for b in range(8):
            ht = hin[b % 2]
            yt = yst[b % 2]
            P.dma(ht[:, :, :], hblk(g, b))
            i = 0
            for j in range(4):
                for k4 in range(4):
                    ps = g.bank[i % 8]
                    for kk in range(4):
                        kc = k4 * 4 + kk
                        P.tr(ps[:, kk * 128:(kk + 1) * 128], ht[:, kc, j * 128:(j + 1) * 128], g.ident_f[:, :])
                    P.cp(yt[:, j, k4 * 512:(k4 + 1) * 512], ps[:, :], eng=("act" if i % 2 else "dve"))
                    i += 1
            P.dma(V(g.y.ap[b * 512:(b + 1) * 512, :].rearrange("(j p) d -> p j d", p=128), g.y.bufs), yt[:, :, :], q="pool")


def phase_inproj(P, g, layer, ncol):
    j = layer // 2
    is_ab = (layer % 2 == 0)
    w = g.ab_w_in[j] if is_ab else g.c_w_in[j]
    TB = 1024
    with P.scope():
        hin = [P.sb([128, KC, 512], F32, "hin") for _ in range(1)]
        sq = P.sb([128, KC, 512], BF16, "sq")
        rstd = P.sb([128, 512], F32, "rstd")
        xT = [P.sb([128, KC, TB], BF16, "xT") for _ in range(2)]
        wb = [P.sb([128, KC, 512], BF16, "wb") for _ in range(2)]
        ost = [P.sb([128, 512], F32, "ost") for _ in range(3)]
        obf = [P.sb([128, 512], BF16, "obf") for _ in range(3)]
        vst = [P.sb([128, 4 * 129], BF16, "vst") for _ in range(2)]
        for v_ in vst:
            P.memset(v_[:, :], 1.0)
        if not is_ab:
            cost = P.sb([32, S], F32, "cost")
            sint = P.sb([32, S], F32, "sint")
            P.dma(cost[:, :], g.c_cos[:, :])
            P.dma(sint[:, :], g.c_sin[:, :])
            rt1 = [P.sb([32, 512], F32, "rt1") for _ in range(2)]
            rt2 = [P.sb([32, 512], F32, "rt2") for _ in range(2)]
        ps_stat = g.bank[0]
        obanks = [g.bank[1], g.bank[2], g.bank[3], g.bank[4]]
        rbanks = [g.bank[5], g.bank[6]]
        cnt = {"o": 0, "w": 0, "s": 0, "v": 0, "r": 0}

        def nextbank():
            b = obanks[cnt["o"] % 4]
            cnt["o"] += 1
            return b

        def loadw(c0, cn):
            t = wb[cnt["w"] % 2]
            cnt["w"] += 1
            P.dma(t[:, :, 0:cn], wsrc(w, c0, cn), q="pool")
            return t

        for tb in range(S // TB):
            x = xT[tb % 2]
            t0 = tb * TB
            for hb in range(2):
                hi = hin[0]
                P.dma(hi[:, :, :], hblk(g, tb * 2 + hb))
                norm_stats(P, g, hi[:, :, :], 512, rstd[:, :], sq, ps_stat)
                norm_apply(P, hi[:, :, :], ncol, rstd[:, :], x[:, :, hb * 512:(hb + 1) * 512])

            def fm_group(c0, epi):
                wt = loadw(c0, 512)
                for c in range(4):
                    for half in range(2):
                        ps = nextbank()
                        for kc in range(KC):
                            P.mm(ps[:, :], wt[:, kc, c * 128:(c + 1) * 128], x[:, kc, half * 512:(half + 1) * 512],
                                 start=(kc == 0), stop=(kc == KC - 1))
                        epi(ps, c, half)

            def tm_group(c0, cn, epi):
                wt = loadw(c0, cn)
                for tt in range(TB // 128):
                    ps = nextbank()
                    for kc in range(KC):
                        P.mm(ps[:, 0:cn], x[:, kc, tt * 128:(tt + 1) * 128], wt[:, kc, 0:cn],
                             start=(kc == 0), stop=(kc == KC - 1))
                    epi(ps, tt)

            def stage():
                i = cnt["s"] % 3
                cnt["s"] += 1
                return ost[i], obf[i]

            if is_ab:
                for grp in range(6):
                    def epi(ps, c, half, grp=grp):
                        of, _ = stage()
                        P.cp(of[:, :], ps[:, :], eng="act")
                        r0 = grp * 512 + c * 128
                        P.dma(g.qkvaT[r0:r0 + 128, 2 + t0 + half * 512: 2 + t0 + half * 512 + 512], of[:, :])
                    fm_group(grp * 512, epi)
                for grp in range(2):
                    def epi(ps, tt, grp=grp):
                        of, _ = stage()
                        P.cp(of[:, :], ps[:, :], eng="act")
                        P.dma(g.zbuf[t0 + tt * 128:t0 + tt * 128 + 128, grp * 512:(grp + 1) * 512], of[:, :])
                    tm_group(3072 + grp * 512, 512, epi)

                def epi(ps, tt):
                    of, _ = stage()
                    P.cp(of[:, 0:32], ps[:, 0:32], eng="act")
                    P.dma(g.gates[t0 + tt * 128:t0 + tt * 128 + 128, :], of[:, 0:32])
                tm_group(4096, 32, epi)
                for grp in range(4):
                    def epi(ps, c, half, grp=grp):
                        _, ob = stage()
                        P.cp(ob[:, :], ps[:, :], eng="act")
                        r0 = grp * 512 + c * 128
                        P.dma(g.qkbT[r0:r0 + 128, t0 + half * 512: t0 + half * 512 + 512], ob[:, :])
                    fm_group(4128 + grp * 512, epi)
                for grp in range(2):
                    def epi(ps, tt, grp=grp):
                        vs = vst[cnt["v"] % 2]
                        cnt["v"] += 1
                        P.cp(vs.v(vs.ap.rearrange("p (h e) -> p h e", e=129)[:, :, 0:128]),
                             ps.v(ps.ap.rearrange("p (h e) -> p h e", e=128)), eng="act")
                        P.dma(g.vb[t0 + tt * 128:t0 + tt * 128 + 128, grp * 516:(grp + 1) * 516], vs[:, :])
                    tm_group(6176 + grp * 512, 512, epi)
            else:
                import os
                DBG = os.environ.get("DBG", "")
                for grp in range(0 if "noqk" in DBG else 8):
                    def epi(ps, c, half, grp=grp):
                        _, ob = stage()
                        P.cp(ob[:, :], ps[:, :], eng="act")
                        if "norope" in DBG:
                            r0 = grp * 512 + c * 128
                            tok = slice(t0 + half * 512, t0 + half * 512 + 512)
                            P.dma(g.qkT[r0:r0 + 128, tok], ob[:, :])
                            return
                        rb = rbanks[cnt["r"] % 2]
                        a1 = rt1[cnt["r"] % 2]
                        a2 = rt2[cnt["r"] % 2]
                        cnt["r"] += 1
                        P.mm(rb[:, :], g.permT[:, :], ob[:, :])
                        tok = slice(t0 + half * 512, t0 + half * 512 + 512)
                        P.tt(a1[:, :], ps[0:32, :], cost[:, tok], ALU.mult)
                        P.tt(a2[:, :], rb[0:32, :], sint[:, tok], ALU.mult)
                        P.tt(ob[0:32, :], a1[:, :], a2[:, :], ALU.add, eng="dve")
                        r0 = grp * 512 + c * 128
                        P.dma(g.qkT[r0:r0 + 128, tok], ob[:, :])
                    fm_group(grp * 512, epi)
                for grp in range(0 if "nov" in DBG else 4):
                    def epi(ps, tt, grp=grp):
                        vs = vst[cnt["v"] % 2]
                        cnt["v"] += 1
                        P.cp(vs.v(vs.ap[:, 0:514].rearrange("p (h e) -> p h e", e=257)[:, :, 0:256]),
                             ps.v(ps.ap.rearrange("p (h e) -> p h e", e=256)), eng="act")
                        P.dma(g.vc[t0 + tt * 128:t0 + tt * 128 + 128, grp * 514:(grp + 1) * 514], vs[:, 0:514])
                    tm_group(4096 + grp * 512, 512, epi)


def phase_outproj(P, g, layer, ncol):
    j = layer // 2
    w = g.ab_w_out[j] if layer % 2 == 0 else g.c_w_out[j]
    with P.scope():
        mixb = [P.sb([128, KC, 512], BF16, "mixb") for _ in range(2)]
        mf = P.sb([128, KC, 512], F32, "mf")
        hin = [P.sb([128, KC, 512], F32, "hin") for _ in range(2)]
        sq = P.sb([128, KC, 512], BF16, "sq")
        rstd = P.sb([128, 512], F32, "rstd")
        wb = [P.sb([128, KC, 512], BF16, "wb") for _ in range(3)]
        wc = 0
        for b in range(8):
            mb = mixb[b % 2]
            hi = hin[b % 2]
            P.dma(mb[:, :, :], V(g.mixT.ap[:, b * 512:(b + 1) * 512].rearrange("(kc p) t -> p kc t", p=128), g.mix_blk[b].bufs))
            P.dma(hi[:, :, :], hblk(g, b))
            for grp in range(4):
                wt = wb[wc % 3]
                wc += 1
                P.dma(wt[:, :, :], wsrc(w, grp * 512, 512), q="pool")
                for c in range(4):
                    n = grp * 4 + c
                    ps = g.bank[1 + (n % 6)]
                    for kc in range(KC):
                        P.mm(ps[:, :], wt[:, kc, c * 128:(c + 1) * 128], mb[:, kc, :], start=(kc == 0), stop=(kc == KC - 1))
                    P.cp(mf[:, n, :], ps[:, :], eng="act")
            norm_stats(P, g, mf[:, :, :], 512, rstd[:, :], sq, g.bank[0])
            norm_apply(P, mf[:, :, :], ncol, rstd[:, :], mf[:, :, :])
            P.tt(hi[:, :, :], hi[:, :, :], mf[:, :, :], ALU.add, eng="pool")
            P.dma(hblk(g, b), hi[:, :, :], q="pool")


def phase_ffn1(P, g, layer, ncol):
    w = g.ffn_w_in[layer]
    TB = 1024
    with P.scope():
        cw = P.sb([128, 3, 128], F32, "cw")
        cb = P.sb([128, 128], F32, "cb")
        for i in range(3):
            load_cols(P, g, V(g.ffn_conv_w.ap[layer, i, :].rearrange("(c p) -> c p", p=128), g.ffn_conv_w.bufs), 128, cw[:, i, :])
        load_cols(P, g, V(g.ffn_conv_b.ap[layer, :].rearrange("(c p) -> c p", p=128), g.ffn_conv_b.bufs), 128, cb[:, :])
        hin = [P.sb([128, KC, 512], F32, "hin") for _ in range(1)]
        sq = P.sb([128, KC, 512], BF16, "sq")
        rstd = P.sb([128, 512], F32, "rstd")
        xT = [P.sb([128, KC, TB], BF16, "xT") for _ in range(2)]
        wbg = [P.sb([128, KC, 512], BF16, "wbg") for _ in range(2)]
        wbu = [P.sb([128, KC, 512], BF16, "wbu") for _ in range(2)]
        carry = P.sb([128, 128, 2], F32, "carry")
        P.memset(carry[:, :, :], 0.0)
        hbs = [P.sb([128, 516], F32, "hb") for _ in range(4)]
        accs = [P.sb([128, 512], F32, "acc") for _ in range(4)]
        gel = [P.sb([128, 512], F32, "gel") for _ in range(2)]
        aout = [P.sb([128, 512], BF16, "aout") for _ in range(3)]
        ps_stat = g.bank[0]
        pb = [g.bank[1 + i] for i in range(6)]
        cnt = {"p": 0, "h": 0, "a": 0}

        def conv(ps, chunk, width, first):
            hb = hbs[cnt["h"] % 4]
            acc = accs[cnt["h"] % 4]
            cnt["h"] += 1
            P.cp(hb[:, 0:2], carry[:, chunk, :], eng="act")
            if ps is not None:
                P.cp(hb[:, 2:2 + width], ps, eng="act")
                P.cp(carry[:, chunk, :], hb[:, width:width + 2], eng="act")
            else:
                P.memset(hb[:, 2:2 + width], 0.0, eng="pool")
            P.act(acc[:, 0:width], hb[:, 0:width], AF.Identity, bias=cb[:, chunk:chunk + 1], scale=cw[:, 0, chunk:chunk + 1])
            P.stt(acc[:, 0:width], hb[:, 1:1 + width], cw[:, 1, chunk:chunk + 1], acc[:, 0:width], ALU.mult, ALU.add)
            P.stt(acc[:, 0:width], hb[:, 2:2 + width], cw[:, 2, chunk:chunk + 1], acc[:, 0:width], ALU.mult, ALU.add)
            return acc

        def gate_up(psg, psu, pair, width, tok0, skip_first):
            ag = conv(psg, pair, width, skip_first)
            au = conv(psu, 64 + pair, width, skip_first)
            ge = gel[cnt["a"] % 2]
            ao = aout[cnt["a"] % 3]
            cnt["a"] += 1
            P.act(ge[:, 0:width], ag[:, 0:width], AF.Gelu_apprx_tanh)
            P.tt(ao[:, 0:width], ge[:, 0:width], au[:, 0:width], ALU.mult, eng="dve")
            s = 1 if skip_first else 0
            if width - s == 1:
                P.dma(g.actT[pair * 128:(pair + 1) * 128, tok0 + s: tok0 + width], ao[:, s:width], allow_slow_non_contiguous=True)
            else:
                P.dma(g.actT[pair * 128:(pair + 1) * 128, tok0 + s: tok0 + width], ao[:, s:width])

        for tb in range(S // TB):
            x = xT[tb % 2]
            t0 = tb * TB
            for hb_ in range(2):
                hi = hin[0]
                P.dma(hi[:, :, :], hblk(g, tb * 2 + hb_))
                norm_stats(P, g, hi[:, :, :], 512, rstd[:, :], sq, ps_stat)
                norm_apply(P, hi[:, :, :], ncol, rstd[:, :], x[:, :, hb_ * 512:(hb_ + 1) * 512])
            for grp in range(16):
                idx = tb * 16 + grp
                wg = wbg[idx % 2]
                wu = wbu[idx % 2]
                if idx == 0:
                    P.dma(wg[:, :, :], wsrc(w, 0, 512), q="pool")
                    P.dma(wu[:, :, :], wsrc(w, DFF, 512), q="pool")
                if idx + 1 < (S // TB) * 16:
                    g2 = (idx + 1) % 16
                    P.dma(wbg[(idx + 1) % 2][:, :, :], wsrc(w, g2 * 512, 512), q="pool")
                    P.dma(wbu[(idx + 1) % 2][:, :, :], wsrc(w, DFF + g2 * 512, 512), q="pool")
                for c in range(4):
                    pair = grp * 4 + c
                    for half in range(2):
                        psg = pb[(cnt["p"] * 2) % 6]
                        psu = pb[(cnt["p"] * 2 + 1) % 6]
                        cnt["p"] += 1
                        for kc in range(KC):
                            P.mm(psg[:, :], wg[:, kc, c * 128:(c + 1) * 128], x[:, kc, half * 512:(half + 1) * 512],
                                 start=(kc == 0), stop=(kc == KC - 1))
                        for kc in range(KC):
                            P.mm(psu[:, :], wu[:, kc, c * 128:(c + 1) * 128], x[:, kc, half * 512:(half + 1) * 512],
                                 start=(kc == 0), stop=(kc == KC - 1))
                        tok0 = t0 + half * 512 - 1
                        gate_up(psg[:, :], psu[:, :], pair, 512, tok0, skip_first=(tok0 < 0))
        for pair in range(64):
            gate_up(None, None, pair, 1, S - 1, False)


def phase_ffn2(P, g, layer, ncol):
    w = g.ffn_w_out[layer]
    with P.scope():
        actb = P.sb([128, 64, 512], BF16, "actb")
        actq = [actb.sub((slice(None), slice(q * 16, (q + 1) * 16), slice(None))) for q in range(4)]
        fT = P.sb([128, KC, 512], F32, "fT")
        hin = [P.sb([128, KC, 512], F32, "hin") for _ in range(2)]
        sq = P.sb([128, KC, 512], BF16, "sq")
        rstd = P.sb([128, 512], F32, "rstd")
        wk = [P.sb([128, 1024], BF16, "wk") for _ in range(4)]
        wc = 0
        for b in range(8):
            hi = hin[b % 2]
            P.dma(hi[:, :, :], hblk(g, b))
            for q in range(4):
                P.dma(actq[q][:, :, :], V(g.actT.ap[q * 2048:(q + 1) * 2048, b * 512:(b + 1) * 512].rearrange("(kc p) t -> p kc t", p=128), g.actT.bufs))
            for nh in range(2):
                for kc in range(64):
                    wt = wk[wc % 4]
                    wc += 1
                    P.dma(wt[:, :], w[kc * 128:(kc + 1) * 128, nh * 1024:(nh + 1) * 1024], q="pool")
                    for n in range(8):
                        P.mm(g.bank[n][:, :], wt[:, n * 128:(n + 1) * 128], actq[kc // 16][:, kc % 16, :],
                             start=(kc == 0), stop=(kc == 63))
                for n in range(8):
                    P.cp(fT[:, nh * 8 + n, :], g.bank[n][:, :], eng=("act" if n % 2 else "dve"))
            norm_stats(P, g, fT[:, :, :], 512, rstd[:, :], sq, g.bank[0])
            norm_apply(P, fT[:, :, :], ncol, rstd[:, :], fT[:, :, :])
            P.tt(hi[:, :, :], hi[:, :, :], fT[:, :, :], ALU.add, eng="pool")
            P.dma(hblk(g, b), hi[:, :, :], q="pool")


def phase_ple(P, g, layer):
    wg = g.ple_w_gate[layer]
    wp = g.ple_w_proj[layer]
    with P.scope():
        hin = [P.sb([128, KC, 512], F32, "hin") for _ in range(2)]
        hb16 = P.sb([128, KC, 512], BF16, "hb16")
        pin = [P.sb([128, 4, PLE], F32, "pin") for _ in range(2)]
        pT = P.sb([128, 2, 512], BF16, "pT")
        ppf = P.sb([128, 8, 512], F32, "ppf")
        sig = [P.sb([128, 512], F32, "sig") for _ in range(3)]
        wk = [P.sb([128, 1024], BF16, "wk") for _ in range(4)]
        wc = 0
        for b in range(8):
            hi = hin[b % 2]
            pi = pin[b % 2]
            P.dma(hi[:, :, :], hblk(g, b))
            P.dma(pi[:, :, :], V(g.p.ap[layer, b * 512:(b + 1) * 512, :].rearrange("(j p) d -> p j d", p=128), g.p.bufs))
            P.cp(hb16[:, :, :], hi[:, :, :], eng="pool")
            for k2 in range(2):
                ps = g.bank[k2]
                for jj in range(4):
                    P.tr(ps[:, jj * 128:(jj + 1) * 128], pi[:, jj, k2 * 128:(k2 + 1) * 128], g.ident_f[:, :])
                P.cp(pT[:, k2, :], ps[:, :], eng="act")
            for nh in range(2):
                for k2 in range(2):
                    wt = wk[wc % 4]
                    wc += 1
                    P.dma(wt[:, :], wp[k2 * 128:(k2 + 1) * 128, nh * 1024:(nh + 1) * 1024], q="pool")
                    for n in range(8):
                        P.mm(g.bank[n][:, :], wt[:, n * 128:(n + 1) * 128], pT[:, k2, :], start=(k2 == 0), stop=(k2 == 1))
                for n in range(8):
                    P.cp(ppf[:, n, :], g.bank[n][:, :], eng=("act" if n % 2 else "dve"))
                for kc in range(KC):
                    wt = wk[wc % 4]
                    wc += 1
                    P.dma(wt[:, :], wg[kc * 128:(kc + 1) * 128, nh * 1024:(nh + 1) * 1024], q="pool")
                    for n in range(8):
                        P.mm(g.bank[n][:, :], wt[:, n * 128:(n + 1) * 128], hb16[:, kc, :], start=(kc == 0), stop=(kc == KC - 1))
                for n in range(8):
                    sg = sig[n % 3]
                    P.act(sg[:, :], g.bank[n][:, :], AF.Sigmoid)
                    P.tt(sg[:, :], sg[:, :], ppf[:, n, :], ALU.mult)
                    P.tt(hi[:, nh * 8 + n, :], hi[:, nh * 8 + n, :], sg[:, :], ALU.add, eng="pool")
            P.dma(hblk(g, b), hi[:, :, :], q="pool")

NEG = -30000.0


def na_classes():
    cls_of = {}
    tables = []
    for r0 in range(0, 64, 2):
        key = r0 if r0 in (0, 2, 60, 62) else -1
        if key in cls_of:
            continue
        cls_of[key] = len(tables)
        rb = min(max(r0 - 4, 0), 54)
        ent = []
        for jj in range(5):
            for rp in range(2):
                r = r0 + rp
                rs = min(max(r - 4, 0), 56)
                e = (rb - r0) + 8 + 2 * jj - rp
                val = [1 if rs <= rb + 2 * jj + i2 < rs + 8 else 0 for i2 in (0, 1)]
                ent.append((min(max(e, 0), 16), val[0], val[1]))
        tables.append(ent)
    return cls_of, tables


def na_host_consts():
    cls_of, tables = na_classes()
    c = np.arange(64)
    col_start = np.clip(c - 8, 0, 48)
    col_in = (c[None, :] >= col_start[:, None]) & (c[None, :] < col_start[:, None] + 16)
    maskT = np.where(col_in.T, 0.0, NEG).astype(np.float32)
    maskT = np.concatenate([maskT, maskT], 0)
    rowm = np.zeros((128, len(tables) * 10), np.float32)
    for ci, ent in enumerate(tables):
        for idx, (e, v0, v1) in enumerate(ent):
            rowm[:64, ci * 10 + idx] = 0.0 if v0 else NEG
            rowm[64:, ci * 10 + idx] = 0.0 if v1 else NEG
    return {"c_namask": maskT, "c_narow": rowm}


def rpb_pad(rpb):
    pad = np.full((2, 8, 18, 127), NEG, np.float32)
    pad[:, :, 1:16, 48:79] = rpb[:, :, :, ::-1]
    kc = np.arange(64)[:, None]
    c = np.arange(64)[None, :]
    pos = c - kc + 63
    out = np.empty((2, 8, 2, 64, 17, 64), np.float32)
    for i2 in range(2):
        sub = pad[:, :, i2:i2 + 17, :]
        out[:, :, i2] = np.transpose(sub[:, :, :, pos], (0, 1, 3, 2, 4))
    return np.ascontiguousarray(out.reshape(2, 8, 128, 17, 64))


def phase_na(P, g, layer):
    j = layer // 2
    scale = 128 ** -0.5
    cls_of, tables = na_classes()
    with P.scope():
        maskT = P.sb([128, 64], F32, "namask")
        rowm = P.sb([128, 50], F32, "narow")
        P.dma(maskT[:, :], g.c_namask[:, :])
        P.dma(rowm[:, :], g.c_narow[:, :])
        Wh = [P.sb([128, 17, 64], F32, "Wh") for _ in range(2)]
        bias = [P.sb([128, 5, 5, 128], F32, "nabias") for _ in range(2)]
        qTs = [P.sb([128, S], BF16, "nq") for _ in range(2)]
        kTs = [P.sb([128, S], BF16, "nk") for _ in range(2)]
        Vs = [P.sb([128, 32, 129], BF16, "nv") for _ in range(2)]
        sc = [P.sb([128, 640], F32, "nsc") for _ in range(2)]
        Eb = [P.sb([128, 5, 128], BF16, "nE") for _ in range(2)]
        ob = [P.sb([128, 128], F32, "nob") for _ in range(2)]
        rz = [P.sb([128, 1], F32, "nrz") for _ in range(2)]
        mst = [P.sb([128, 512], BF16, "nmst") for _ in range(2)]
        npair = 0
        for h in range(8):
            W = Wh[h % 2]
            bs = bias[h % 2]
            qT = qTs[h % 2]
            kT = kTs[h % 2]
            Vb = Vs[h % 2]
            P.dma(W[:, :, :], g.rpbpad[j, h, :, :, :])
            mb = bass.AP(tensor=maskT.ap.tensor, offset=maskT.ap.offset,
                         ap=[list(maskT.ap.ap[0]), [0, 17], [1, 64]])
            P.tt(W[:, :, :], W[:, :, :], V(mb, maskT.bufs), ALU.add, eng="pool")
            for ci, ent in enumerate(tables):
                for idx, (e, v0, v1) in enumerate(ent):
                    jj, rp = idx // 2, idx % 2
                    P.ts(bs[:, ci, jj, rp * 64:(rp + 1) * 64], W[:, e, :], rowm[:, ci * 10 + idx: ci * 10 + idx + 1], ALU.add,
                         eng="pool")
            P.dma(qT[:, :], g.qkbT[h * 128:(h + 1) * 128, :])
            P.dma(kT[:, :], g.qkbT[1024 + h * 128:1024 + (h + 1) * 128, :])
            P.dma(Vb[:, :, :], V(g.vb.ap[:, h * 129:(h + 1) * 129].rearrange("(kc p) e -> p kc e", p=128), g.vb.bufs))
            for r0 in range(0, 64, 2):
                ci = cls_of[r0 if r0 in (0, 2, 60, 62) else -1]
                rb = min(max(r0 - 4, 0), 54)
                pa = g.bank[(npair % 2) * 2]
                pb = g.bank[(npair % 2) * 2 + 1]
                po = g.bank[4 + npair % 2]
                tps = g.bank[6 + (npair // 4) % 2]
                s_ = sc[npair % 2]
                E = Eb[npair % 2]
                o_ = ob[npair % 2]
                z_ = rz[npair % 2]
                qs = slice(r0 * 64, r0 * 64 + 128)
                for jj in range(5):
                    kt0 = (rb + 2 * jj) * 64
                    dst = pa[:, jj * 128:(jj + 1) * 128] if jj < 4 else pb[:, 0:128]
                    P.mm(dst, kT[:, kt0:kt0 + 128], qT[:, qs])
                P.stt(s_[:, 0:512], pa[:, :], scale, bs.v(bs.ap[:, ci, 0:4, :].rearrange("p a b -> p (a b)")), ALU.mult, ALU.add)
                P.stt(s_[:, 512:640], pb[:, 0:128], scale, bs[:, ci, 4, :], ALU.mult, ALU.add)
                P.act(E.v(E.ap.rearrange("p a b -> p (a b)")), s_[:, :], AF.Exp)
                for jj in range(5):
                    P.mm(po[:, 0:129], E[:, jj, :], Vb[:, rb // 2 + jj, :], start=(jj == 0), stop=(jj == 4))
                P.recip(z_[:, :], po[:, 128:129])
                P.ts(o_[:, :], po[:, 0:128], z_[:, 0:1], ALU.mult)
                slot = npair % 4
                P.tr(tps[:, slot * 128:(slot + 1) * 128], o_[:, :], g.ident_f[:, :])
                if slot == 3:
                    ms = mst[(npair // 4) % 2]
                    P.cp(ms[:, :], tps[:, :], eng="act")
                    t0 = (r0 - 6) * 64
                    P.dma(g.mixT[1024 + h * 128:1024 + (h + 1) * 128, t0:t0 + 512], ms[:, :], q="pool")
                npair += 1


def phase_mixer_ab(P, g, layer):
    phase_delta(P, g, layer)
    phase_na(P, g, layer)

def delta_host_consts():
    m = np.arange(128)
    triF = (m[:, None] <= m[None, :]).astype(np.float32)
    triB = (m[:, None] >= m[None, :]).astype(np.float32)
    strF = (m[:, None] < m[None, :]).astype(np.float32)
    strB = (m[:, None] > m[None, :]).astype(np.float32)
    blk = [(m[:, None] // 16 == m[None, :] // 16).astype(np.float32)]
    for sz in (16, 32, 64):
        blk.append(((m[:, None] // (2 * sz) == m[None, :] // (2 * sz)) & (m[:, None] // sz != m[None, :] // sz)).astype(np.float32))
    return {"c_tri": np.stack([triF, triB], 0), "c_mstrict": np.stack([strF, strB], 0), "c_blkm": np.stack(blk, 0)}


def phase_delta(P, g, layer):
    j = layer // 2
    with P.scope():
        tri = P.sb([128, 2, 128], F32, "tri")
        mstr = P.sb([128, 2, 128], F32, "mstr")
        blkm = P.sb([128, 4, 128], F32, "blkm")
        for d in range(4):
            P.dma(blkm[:, d, :], g.c_blkm[d, :, :])
        ones_f = P.sb([128, 128], F32, "ones_f")
        onec = P.sb([128, 1], F32, "onec")
        zpad = P.sb([128, 2], F32, "zpad")
        onw = P.sb([128, 1, 128], F32, "onw")
        cwA = P.sb([128, 5, 24], F32, "cwA")
        for d in range(2):
            P.dma(tri[:, d, :], g.c_tri[d, :, :])
            P.dma(mstr[:, d, :], g.c_mstrict[d, :, :])
        P.memset(ones_f[:, :], 1.0)
        P.memset(onec[:, :], 1.0)
        P.memset(zpad[:, :], 0.0)
        P.dma(onw[:, :, :], V(g.ab_out_norm_full.ap[j:j + 1, :].partition_broadcast(128), g.ab_out_norm_full.bufs))
        for c in range(24):
            P.dma(g.qkvaT[c * 128:(c + 1) * 128, 0:2], zpad[:, :])
            P.dma(g.qkvaT[c * 128:(c + 1) * 128, S + 2:S + 4], zpad[:, :])
        for i in range(5):
            load_cols(P, g, V(g.ab_conv_w_full.ap[j, i, :].rearrange("(c p) -> c p", p=128), g.ab_conv_w_full.bufs), 24, cwA[:, i, :])

        def gt(name):
            return P.sb([128, 32, 16], F32, name)
        gdec, beta, negb, gc, egc, kdsc, eG, Gt = [gt(n) for n in ("gdec", "beta", "negb", "gc", "egc", "kdsc", "eG", "Gt")]
        with P.scope():
            Gm = P.sb([128, 32, 32], F32, "Gm")
            alb = P.sb([128, 1, 16], F32, "alb")
            dtb = P.sb([128, 1, 16], F32, "dtb")
            P.dma(Gm[:, :, :], V(g.gates.ap.rearrange("(t p) f -> p t f", p=128), g.gates.bufs))
            P.dma(alb[:, :, :], V(g.ab_a_log_full.ap[j:j + 1].rearrange("o a b -> o (a b)").partition_broadcast(128), g.ab_a_log_full.bufs))
            P.dma(dtb[:, :, :], V(g.ab_dt_bias_full.ap[j:j + 1].rearrange("o a b -> o (a b)").partition_broadcast(128), g.ab_dt_bias_full.bufs))

            def bc16(t):
                return V(bass.AP(tensor=t.ap.tensor, offset=t.ap.offset, ap=[list(t.ap.ap[0]), [0, 32], [1, 16]]), t.bufs)
            P.tt(gdec[:, :, :], Gm[:, :, 0:16], bc16(dtb), ALU.add)
            P.act(gdec[:, :, :], gdec[:, :, :], AF.Exp)
            P.act(gdec[:, :, :], gdec[:, :, :], AF.Ln, bias=onec[:, 0:1], scale=1.0)
            P.act(alb[:, :, :], alb[:, :, :], AF.Exp)
            P.ts(alb[:, :, :], alb[:, :, :], -1.0, ALU.mult)
            P.tt(gdec[:, :, :], gdec[:, :, :], bc16(alb), ALU.mult)
            P.act(beta[:, :, :], Gm[:, :, 16:32], AF.Sigmoid)
            P.ts(negb[:, :, :], beta[:, :, :], -1.0, ALU.mult)
            g2 = gdec.v(gdec.ap.rearrange("p t f -> p (t f)"))
            P.mm(g.bank[0][:, :], tri[:, 0, :], g2)
            P.mm(g.bank[1][:, :], tri[:, 1, :], g2)
            P.mm(g.bank[2][:, :], ones_f[:, :], g2)

            def b3(bk):
                return bk.v(bk.ap.rearrange("p (t f) -> p t f", f=16))
            P.cp(gc[:, :, 0:8], b3(g.bank[0])[:, :, 0:8], eng="act")
            P.cp(gc[:, :, 8:16], b3(g.bank[1])[:, :, 8:16], eng="act")
            P.cp(Gt[:, :, :], b3(g.bank[2]), eng="act")
            P.act(egc[:, :, :], gc[:, :, :], AF.Exp)
            P.act(eG[:, :, :], Gt[:, :, :], AF.Exp)
            P.tt(kdsc[:, :, :], Gt[:, :, :], gc[:, :, :], ALU.subtract)
            P.act(kdsc[:, :, :], kdsc[:, :, :], AF.Exp)

        qT = P.sb([128, S], BF16, "dqT")
        kT = P.sb([128, S], BF16, "dkT")
        ktok = P.sb([128, 32, 128], BF16, "ktok")
        vtok = P.sb([128, 32, 128], BF16, "vtok")
        oacc = P.sb([128, 32, 128], F32, "oacc")
        oacc_t = [oacc.sub((slice(None), t, slice(None))) for t in range(32)]
        oall = V(oacc.ap, [b for t in oacc_t for b in t.bufs])
        ost = [P.sb([128, 512], BF16, "dost") for _ in range(2)]
        ss32 = P.sb([128, 32], F32, "ss32")

        def f32t(n):
            return P.sb([128, 128], F32, n)

        def b16t(n):
            return P.sb([128, 128], BF16, n)
        gbc = [f32t("gbc") for _ in range(8)]
        kd = [b16t("kd") for _ in range(8)]
        kg = [b16t("kg") for _ in range(8)]
        tT = [f32t("tT") for _ in range(8)]
        Egc = [f32t("Egc") for _ in range(8)]
        DTs = [f32t("DTs") for _ in range(8)]
        DTi = [f32t("DTi") for _ in range(8)]
        Pb = [[f32t("Pb") for _ in range(2)] for _ in range(8)]
        Qb = [[f32t("Qb") for _ in range(2)] for _ in range(8)]
        Rb = [[f32t("Rb") for _ in range(2)] for _ in range(8)]
        Rt = [[f32t("Rt") for _ in range(2)] for _ in range(8)]
        NF = [f32t("NF") for _ in range(8)]
        QF = [f32t("QF") for _ in range(8)]
        NO = [f32t("NO") for _ in range(8)]
        NOT = [f32t("NOT") for _ in range(8)]
        Wb = [f32t("Wb") for _ in range(8)]
        Wpb = [f32t("Wpb") for _ in range(8)]
        TTb = [b16t("TTb") for _ in range(8)]
        tmpq = [f32t("tmpq") for _ in range(8)]
        intraT = [[b16t("intraT") for _ in range(8)] for _ in range(1)]
        QpT = [[f32t("QpT") for _ in range(8)] for _ in range(1)]
        McT = [[f32t("McT") for _ in range(8)] for _ in range(1)]
        Bc = [[f32t("Bc") for _ in range(8)] for _ in range(1)]
        uw = [[P.sb([128, 256], BF16, "uw") for _ in range(8)] for _ in range(1)]
        St = [[f32t("St") for _ in range(2)] for _ in range(2)]
        bk = g.bank

        def slot(b, ii):
            bb = b if ii < 4 else (b + 4) % 8
            return bk[bb][:, (ii % 4) * 128:(ii % 4 + 1) * 128]

        for h in range(8):
            nb = 0
            pre_scope = P.scope()
            pre_scope.__enter__()
            xin = [P.sb([128, 516], F32, "dxin") for _ in range(2)]
            cacc = [P.sb([128, 512], F32, "cacc") for _ in range(2)]
            ysil = [P.sb([128, 512], F32, "ysil") for _ in range(2)]
            sqb = P.sb([128, 512], BF16, "dsq")
            rn = P.sb([128, 512], F32, "drn")
            for blk in range(8):
                t0 = blk * 512
                for which in range(3):
                    c = which * 8 + h
                    xi = xin[nb % 2]
                    ac = cacc[nb % 2]
                    ys = ysil[nb % 2]
                    nb += 1
                    P.dma(xi[:, :], g.qkvaT[c * 128:(c + 1) * 128, t0:t0 + 516])
                    P.act(ac[:, :], xi[:, 0:512], AF.Identity, scale=cwA[:, 0, c:c + 1])
                    for i in range(1, 5):
                        P.stt(ac[:, :], xi[:, i:i + 512], cwA[:, i, c:c + 1], ac[:, :], ALU.mult, ALU.add)
                    P.act(ys[:, :], ac[:, :], AF.Silu)
                    if which < 2:
                        P.tt(sqb[:, :], ys[:, :], ys[:, :], ALU.mult, eng="pool")
                        P.mm(bk[0][:, :], g.ones_b[:, :], sqb[:, :])
                        P.act(rn[:, :], bk[0][:, :], AF.Sqrt, bias=g.epsc[:, 0:1], scale=1.0)
                        P.recip(rn[:, :], rn[:, :])
                    if which == 0:
                        P.stt(qT[:, t0:t0 + 512], ys[:, :], 128 ** -0.5, rn[:, :], ALU.mult, ALU.mult)
                    else:
                        if which == 1:
                            P.tt(ys[:, :], ys[:, :], rn[:, :], ALU.mult)
                            P.cp(kT[:, t0:t0 + 512], ys[:, :], eng="pool")
                        pb_ = bk[1 + which]
                        for jj in range(4):
                            P.tr(pb_[:, jj * 128:(jj + 1) * 128], ys[:, jj * 128:(jj + 1) * 128], g.ident_f[:, :])
                        dst = ktok if which == 1 else vtok
                        P.cp(dst.v(dst.ap[:, blk * 4:(blk + 1) * 4, :].rearrange("p a b -> p (a b)")), pb_[:, :], eng="act")
            pre_scope.__exit__(None, None, None)
            P.memset(oall, 0.0)
            for d in range(2):
                P.memset(St[d][0][:, :], 0.0)

            spar = [0, 0]
            for grp in range(8):
                par = 0
                insts = []
                for q_ in range(4):
                    insts.append((4 * grp + q_, 0))
                    insts.append((31 - 4 * grp - q_, 1))

                def col(X, ii):
                    t, d = insts[ii]
                    return X[:, t, d * 8 + h: d * 8 + h + 1]
                for ii, (t, d) in enumerate(insts):
                    P.ts(gbc[ii][:, :], ones_f[:, :], col(gdec, ii), ALU.mult, eng="pool")
                    P.ts(kd[ii][:, :], ktok[:, t, :], col(kdsc, ii), ALU.mult)
                    P.ts(kg[ii][:, :], ktok[:, t, :], col(egc, ii), ALU.mult)
                for ii, (t, d) in enumerate(insts):
                    ts_ = slice(t * 128, (t + 1) * 128)
                    P.mm(slot(0, ii), gbc[ii][:, :], tri[:, d, :])
                    P.mm(slot(1, ii), kT[:, ts_], kT[:, ts_])
                    P.mm(slot(2, ii), kT[:, ts_], qT[:, ts_])
                for ii, (t, d) in enumerate(insts):
                    P.ts(tT[ii][:, :], slot(0, ii), col(gc, ii), ALU.subtract, 0.0, ALU.min)
                    P.act(Egc[ii][:, :], slot(0, ii), AF.Exp)
                    P.act(tT[ii][:, :], tT[ii][:, :], AF.Exp)
                    P.tt(DTs[ii][:, :], tT[ii][:, :], mstr[:, d, :], ALU.mult, eng="pool")
                    P.tt(DTi[ii][:, :], tT[ii][:, :], tri[:, d, :], ALU.mult, eng="pool")
                    P.stt(NF[ii][:, :], slot(1, ii), col(negb, ii), DTs[ii][:, :], ALU.mult, ALU.mult)
                    P.tt(intraT[par][ii][:, :], slot(2, ii), DTi[ii][:, :], ALU.mult)
                    P.tt(Pb[ii][0][:, :], NF[ii][:, :], blkm[:, 0, :], ALU.mult, eng="pool")
                    P.tt(Rb[ii][0][:, :], Pb[ii][0][:, :], g.ident_f[:, :], ALU.add, eng="pool")
                for ii in range(8):
                    P.tr(slot(3, ii), NF[ii][:, :], g.ident_f[:, :])
                for ii in range(8):
                    P.cp(QF[ii][:, :], slot(3, ii), eng="act")
                    P.tt(Qb[ii][0][:, :], QF[ii][:, :], blkm[:, 0, :], ALU.mult, eng="pool")
                    P.tt(Rt[ii][0][:, :], Qb[ii][0][:, :], g.ident_f[:, :], ALU.add, eng="pool")
                for k in range(1, 5):
                    a, b_ = (k - 1) % 2, k % 2
                    for ii in range(8):
                        if k <= 2:
                            P.mm(slot(4, ii), Qb[ii][a][:, :], Pb[ii][a][:, :])
                        if k <= 3:
                            P.mm(slot(5, ii), Pb[ii][a][:, :], Qb[ii][a][:, :])
                        if k >= 2:
                            P.mm(slot(6, ii), Qb[ii][a][:, :], Rb[ii][b_][:, :])
                            P.mm(slot(3, ii), Rb[ii][b_][:, :], Qb[ii][a][:, :])
                    for ii in range(8):
                        if k >= 2:
                            P.tt(Rb[ii][a][:, :], slot(6, ii), Rb[ii][b_][:, :], ALU.add)
                            P.tt(Rt[ii][a][:, :], slot(3, ii), Rt[ii][b_][:, :], ALU.add)
                        if k <= 2:
                            P.cp(Pb[ii][b_][:, :], slot(4, ii), eng="act")
                        if k <= 3:
                            P.cp(Qb[ii][b_][:, :], slot(5, ii), eng="act")
                cur = 1
                for lv in range(3):
                    nxt = 1 - cur
                    for ii in range(8):
                        P.tt(NOT[ii][:, :], QF[ii][:, :], blkm[:, 1 + lv, :], ALU.mult, eng="pool")
                        if lv < 2:
                            P.tt(NO[ii][:, :], NF[ii][:, :], blkm[:, 1 + lv, :], ALU.mult, eng="pool")
                    for ii in range(8):
                        P.mm(slot(4, ii), NOT[ii][:, :], Rb[ii][cur][:, :])
                        if lv < 2:
                            P.mm(slot(5, ii), NO[ii][:, :], Rt[ii][cur][:, :])
                    for ii in range(8):
                        P.cp(Wb[ii][:, :], slot(4, ii), eng="act")
                        if lv < 2:
                            P.cp(Wpb[ii][:, :], slot(5, ii), eng="dve")
                    for ii in range(8):
                        P.mm(slot(6, ii), Rt[ii][cur][:, :], Wb[ii][:, :])
                        if lv < 2:
                            P.mm(slot(3, ii), Rb[ii][cur][:, :], Wpb[ii][:, :])
                    for ii in range(8):
                        if lv < 2:
                            P.tt(Rb[ii][nxt][:, :], slot(6, ii), Rb[ii][cur][:, :], ALU.add)
                            P.tt(Rt[ii][nxt][:, :], slot(3, ii), Rt[ii][cur][:, :], ALU.add)
                        else:
                            P.tt(TTb[ii][:, :], slot(6, ii), Rb[ii][cur][:, :], ALU.add)
                    cur = nxt
                for ii, (t, d) in enumerate(insts):
                    pbk = bk[ii // 2]
                    o_ = (ii % 2) * 256
                    P.mm(pbk[:, o_:o_ + 128], TTb[ii][:, :], vtok[:, t, :])
                    P.mm(pbk[:, o_ + 128:o_ + 256], TTb[ii][:, :], kg[ii][:, :])
                for ii, (t, d) in enumerate(insts):
                    pbk = bk[ii // 2]
                    o_ = (ii % 2) * 256
                    P.ts(uw[par][ii][:, :], pbk[:, o_:o_ + 256], col(beta, ii), ALU.mult)
                for ii, (t, d) in enumerate(insts):
                    P.mm(slot(2, ii), uw[par][ii][:, 128:256], intraT[par][ii][:, :])
                    P.mm(slot(3, ii), uw[par][ii][:, 128:256], kd[ii][:, :])
                    P.mm(slot(5, ii), kd[ii][:, :], uw[par][ii][:, 0:128])
                for ii, (t, d) in enumerate(insts):
                    ts_ = slice(t * 128, (t + 1) * 128)
                    P.tt(tmpq[ii][:, :], qT[:, ts_], Egc[ii][:, :], ALU.mult, eng="pool")
                    P.tt(QpT[par][ii][:, :], tmpq[ii][:, :], slot(2, ii), ALU.subtract)
                    P.stt(McT[par][ii][:, :], g.ident_f[:, :], col(eG, ii), slot(3, ii), ALU.mult, ALU.subtract)
                    P.cp(Bc[par][ii][:, :], slot(5, ii), eng="act")
                for ii, (t, d) in enumerate(insts):
                    so = St[d][spar[d] % 2]
                    sn = St[d][(spar[d] + 1) % 2]
                    spar[d] += 1
                    ps_s = bk[7][:, d * 128:(d + 1) * 128]
                    ps_o = bk[7][:, 256 + d * 128:256 + (d + 1) * 128]
                    P.mm(ps_s, McT[par][ii][:, :], so[:, :])
                    P.mm(ps_o, intraT[par][ii][:, :], uw[par][ii][:, 0:128], start=True, stop=False)
                    P.mm(ps_o, QpT[par][ii][:, :], so[:, :], start=False, stop=True)
                    P.tt(sn[:, :], ps_s, Bc[par][ii][:, :], ALU.add)
                    P.tt(oacc_t[t][:, :], ps_o, oacc_t[t][:, :], ALU.add)

            post_scope = P.scope()
            post_scope.__enter__()
            zt = P.sb([128, 32, 128], F32, "zt")
            sqbig = P.sb([128, 16, 128], F32, "sqbig")
            P.dma(zt[:, :, :], V(g.zbuf.ap[:, h * 128:(h + 1) * 128].rearrange("(t p) e -> p t e", p=128), g.zbuf.bufs))
            P.act(zt[:, :, :], zt[:, :, :], AF.Silu)
            for half in range(2):
                hs = slice(half * 16, (half + 1) * 16)
                tmp = V(oacc.ap[:, hs, :], oall.bufs)
                P.tt(sqbig[:, :, :], tmp, tmp, ALU.mult, eng="pool")
                P.red(ss32[:, hs], sqbig[:, :, :], ALU.add)
            P.act(ss32[:, :], ss32[:, :], AF.Sqrt, bias=g.epsc[:, 0:1], scale=1.0 / 128)
            P.recip(ss32[:, :], ss32[:, :])
            rb_ = V(bass.AP(tensor=ss32.ap.tensor, offset=ss32.ap.offset, ap=[list(ss32.ap.ap[0]), [1, 32], [0, 128]]), ss32.bufs)
            wb_ = V(bass.AP(tensor=onw.ap.tensor, offset=onw.ap.offset, ap=[list(onw.ap.ap[0]), [0, 32], [1, 128]]), onw.bufs)
            P.tt(oall, oall, rb_, ALU.mult)
            P.tt(oall, oall, wb_, ALU.mult, eng="pool")
            P.tt(oall, oall, zt[:, :, :], ALU.mult)
            for blk in range(8):
                pb_ = bk[blk % 2]
                for jj in range(4):
                    t = blk * 4 + jj
                    P.tr(pb_[:, jj * 128:(jj + 1) * 128], V(oacc.ap[:, t, :], oall.bufs), g.ident_f[:, :])
                os_ = ost[blk % 2]
                P.cp(os_[:, :], pb_[:, :], eng="act")
                P.dma(g.mixT[h * 128:(h + 1) * 128, blk * 512:(blk + 1) * 512], os_[:, :], q="pool")
            post_scope.__exit__(None, None, None)
import math


def phase_mixer_c(P, g, layer):
    j = layer // 2
    lam_init = 0.8 - 0.6 * math.exp(-0.3 * layer)
    scale = 128 ** -0.5
    with P.scope():
        lp = P.sb([128, 4, 128], F32, "lp")
        P.dma(lp[:, :, :], V(g.c_lambda_full.ap[j].partition_broadcast(128), g.c_lambda_full.bufs))
        pr = P.sb([128, 2, 128], F32, "pr")
        P.tt(pr[:, 0, :], lp[:, 0, :], lp[:, 1, :], ALU.mult)
        P.tt(pr[:, 1, :], lp[:, 2, :], lp[:, 3, :], ALU.mult)
        ssum = P.sb([128, 2], F32, "ssum")
        P.red(ssum[:, :], pr[:, :, :], ALU.add)
        P.act(ssum[:, :], ssum[:, :], AF.Exp)
        neglam = P.sb([128, 1], F32, "neglam")
        P.tt(neglam[:, :], ssum[:, 0:1], ssum[:, 1:2], ALU.subtract)
        P.ts(neglam[:, :], neglam[:, :], -1.0, ALU.mult, -lam_init, ALU.add)
        subw = P.sb([128, 256], F32, "subw")
        P.dma(subw[:, :], V(g.c_subln_full.ap[j:j + 1, :].partition_broadcast(128), g.c_subln_full.bufs))
        P.ts(subw[:, :], subw[:, :], 1.0 - lam_init, ALU.mult)

        qTs = [P.sb([128, 2, S], BF16, "qT") for _ in range(2)]
        kTs = [P.sb([128, 2, S], BF16, "kT") for _ in range(2)]
        Vhs = [P.sb([128, 32, 257], BF16, "Vh") for _ in range(2)]
        Es = [P.sb([128, 512], BF16, "E") for _ in range(3)]
        osb = [P.sb([128, 4, 257], F32, "osb") for _ in range(2)]
        rz = P.sb([128, 4], F32, "rz")
        abuf = [P.sb([128, 256], F32, "abuf") for _ in range(2)]
        junk = P.sb([128, 256], F32, "junk")
        mst = [P.sb([128, 2, 512], BF16, "mst") for _ in range(2)]
        obank = [g.bank[i] for i in range(4)]
        sbank = [g.bank[4], g.bank[5], g.bank[6]]
        tbank = g.bank[7]
        nqt = 0
        for h in range(8):
            qT = qTs[h % 2]
            kT = kTs[h % 2]
            Vh = Vhs[h % 2]
            for m in range(2):
                r0 = (h * 2 + m) * 128
                P.dma(qT[:, m, :], g.qkT[r0:r0 + 128, :])
                P.dma(kT[:, m, :], g.qkT[2048 + r0:2048 + r0 + 128, :])
            P.dma(Vh[:, :, :], V(g.vc.ap[:, h * 257:(h + 1) * 257].rearrange("(kc p) e -> p kc e", p=128), g.vc.bufs))
            for qb in range(8):
                qs = slice(qb * 512, (qb + 1) * 512)
                for m in range(2):
                    def smm(kc):
                        P.mm(sbank[kc % 3][:, :], kT[:, m, kc * 128:(kc + 1) * 128], qT[:, m, qs])
                    smm(0)
                    smm(1)
                    for kc in range(32):
                        if kc + 2 < 32:
                            smm(kc + 2)
                        E = Es[kc % 3]
                        P.act(E[:, :], sbank[kc % 3][:, :], AF.Exp, scale=scale)
                        for qt in range(4):
                            P.mm(obank[qt][:, 0:257], E[:, qt * 128:(qt + 1) * 128], Vh[:, kc, :], start=(kc == 0), stop=(kc == 31))
                    for qt in range(4):
                        P.cp(osb[m][:, qt, :], obank[qt][:, 0:257], eng="dve")
                ms = mst[qb % 2]
                for qt in range(4):
                    o1 = osb[0][:, qt, :]
                    o2 = osb[1][:, qt, :]
                    a = abuf[nqt % 2]
                    nqt += 1
                    P.recip(rz[:, 0:1], o1[:, 256:257])
                    P.recip(rz[:, 1:2], o2[:, 256:257])
                    P.tt(rz[:, 1:2], rz[:, 1:2], neglam[:, :], ALU.mult)
                    P.ts(a[:, :], o1[:, 0:256], rz[:, 0:1], ALU.mult)
                    P.stt(a[:, :], o2[:, 0:256], rz[:, 1:2], a[:, :], ALU.mult, ALU.add)
                    P.act(junk[:, :], a[:, :], AF.Square, accum_out=rz[:, 2:3])
                    P.act(rz[:, 3:4], rz[:, 2:3], AF.Sqrt, bias=g.epsc[:, 0:1], scale=1.0 / 256)
                    P.recip(rz[:, 3:4], rz[:, 3:4])
                    P.stt(a[:, :], a[:, :], rz[:, 3:4], subw[:, :], ALU.mult, ALU.mult)
                    for e2 in range(2):
                        P.tr(tbank[:, e2 * 128:(e2 + 1) * 128], a[:, e2 * 128:(e2 + 1) * 128], g.ident_f[:, :])
                    P.cp(ms[:, :, qt * 128:(qt + 1) * 128], tbank.v(tbank.ap[:, 0:256].rearrange("p (e q) -> p e q", e=2)), eng="act")
                P.dma(V(g.mixT.ap[h * 256:(h + 1) * 256, qs].rearrange("(e p) t -> p e t", p=128), g.mixT.bufs), ms[:, :, :], q="pool")

def host_consts():
    c = {}
    c["c_ident"] = np.eye(128, dtype=np.float32)
    inv = 500000.0 ** (-np.arange(0, 32, 2, dtype=np.float32) / 32.0)
    ang = np.arange(S, dtype=np.float32)[:, None] * inv[None, :].astype(np.float32)
    cos = np.cos(ang).astype(np.float32).T
    sin = np.sin(ang).astype(np.float32).T
    c["c_cos"] = np.ascontiguousarray(np.concatenate([cos, cos], 0))
    c["c_sin"] = np.ascontiguousarray(np.concatenate([sin, sin], 0))
    pm = np.zeros((128, 128), np.float32)
    for i in range(16):
        pm[i + 16, i] = -1.0
        pm[i, i + 16] = 1.0
    c["c_permT"] = pm
    c.update(delta_host_consts())
    c.update(na_host_consts())
    return c


WEIGHT_SPECS = [
    ("ab_w_in", [2, D, AB_IN]), ("ab_conv_w", [2, 5, 3072]), ("ab_a_log", [2, 2, 8]), ("ab_dt_bias", [2, 2, 8]),
    ("ab_out_norm", [2, 128]), ("ab_rpb", [2, 8, 15, 31]), ("ab_w_out", [2, D, D]),
    ("c_w_in", [2, D, C_IN]), ("c_lambda", [2, 4, 128]), ("c_subln", [2, 256]), ("c_w_out", [2, D, D]),
    ("norms", [4, 4, D]), ("ffn_w_in", [4, D, 2 * DFF]), ("ffn_conv_w", [4, 3, 2 * DFF]), ("ffn_conv_b", [4, 2 * DFF]),
    ("ffn_w_out", [4, DFF, D]), ("ple_w_proj", [4, PLE, D]), ("ple_w_gate", [4, D, D]),
]


def build_program(stages=None, debug=()):
    nc = bass.Bass("TRN2", target_bir_lowering=False)
    P = Prog(nc)
    g = G()
    g.x = P.dram("x", [S, D], F32, "ExternalInput")
    g.p = P.dram("p", [DEPTH, S, PLE], F32, "ExternalInput")
    g.y = P.dram("y", [S, D], F32, "ExternalOutput")
    for name, shape in WEIGHT_SPECS:
        t = P.dram(name, shape, F32, "ExternalInput")
        setattr(g, name + "_full", t)
    consts = host_consts()
    for name, arr in consts.items():
        setattr(g, name, P.dram(name, list(arr.shape), F32, "ExternalInput"))

    g.rpbpad = P.dram("rpbpad", [2, 8, 128, 17, 64], F32, "ExternalInput")

    def per(name, n):
        full = getattr(g, name + "_full")
        setattr(g, name, [T(full.ap[i], full.bufs) for i in range(n)])
    per("ab_w_in", 2); per("ab_w_out", 2); per("c_w_in", 2); per("c_w_out", 2)
    per("ffn_w_in", 4); per("ffn_w_out", 4); per("ple_w_proj", 4); per("ple_w_gate", 4)
    g.ffn_conv_w = g.ffn_conv_w_full
    g.ffn_conv_b = g.ffn_conv_b_full

    def scratch(name, shape, dt):
        return P.dram(name, shape, dt, "ExternalOutput" if name in debug else "Internal")
    g.hT = scratch("hT", [D, S], F32)
    g.hT_blk = [g.hT.sub((slice(None), slice(b * 512, (b + 1) * 512))) for b in range(8)]
    g.mixT = scratch("mixT", [D, S], BF16)
    g.mix_blk = [g.mixT.sub((slice(None), slice(b * 512, (b + 1) * 512))) for b in range(8)]
    g.actT = scratch("actT", [DFF, S], BF16)
    g.qkvaT = scratch("qkvaT", [3072, S + 4], F32)
    g.zbuf = scratch("zbuf", [S, 1024], F32)
    g.gates = scratch("gates", [S, 32], F32)
    g.qkbT = scratch("qkbT", [2048, S], BF16)
    g.vb = scratch("vb", [S, 8 * 129], BF16)
    g.qkT = scratch("qkT", [4096, S], BF16)
    g.vc = scratch("vc", [S, 8 * 257], BF16)

    g.ident_f = P.sb([128, 128], F32, "ident_f")
    g.ident_b = P.sb([128, 128], BF16, "ident_b")
    g.ones_b = P.sb([128, 128], BF16, "ones_b")
    g.epsc = P.sb([128, 1], F32, "epsc")
    g.permT = P.sb([128, 128], BF16, "permT")
    g.ncols = P.sb([128, 4, KC], F32, "ncols")
    g.bank = [P.ps([128, 512], F32, "bank%d" % i) for i in range(8)]
    P.dma(g.ident_f[:, :], g.c_ident[:, :])
    P.cp(g.ident_b[:, :], g.ident_f[:, :])
    P.memset(g.ones_b[:, :], 1.0)
    P.memset(g.epsc[:, :], EPS)
    with P.scope():
        pf = P.sb([128, 128], F32, "pf")
        P.dma(pf[:, :], g.c_permT[:, :])
        P.cp(g.permT[:, :], pf[:, :])

    st = stages
    def on(s):
        return st is None or s in st
    if on("in"):
        with nc.named_scope("in_tr"):
            phase_in_transpose(P, g)
    for layer in range(DEPTH):
        if not (st is None or ("L%d" % layer) in st):
            continue
        for i in range(4):
            load_cols(P, g, V(g.norms_full.ap[layer, i, :].rearrange("(c p) -> c p", p=128), g.norms_full.bufs), KC, g.ncols[:, i, :])
        if on("inproj"):
            with nc.named_scope("L%d_inproj" % layer):
                phase_inproj(P, g, layer, g.ncols[:, 0, :])
        if on("mixer"):
            if layer % 2 == 0:
                with nc.named_scope("L%d_delta" % layer):
                    phase_delta(P, g, layer)
                with nc.named_scope("L%d_na" % layer):
                    phase_na(P, g, layer)
            else:
                with nc.named_scope("L%d_mixc" % layer):
                    phase_mixer_c(P, g, layer)
        if on("outproj"):
            with nc.named_scope("L%d_outproj" % layer):
                phase_outproj(P, g, layer, g.ncols[:, 1, :])
        if st is not None and "dumph" in st:
            dm = P.dram("dbg_m%d" % layer, [D, S], F32, "ExternalOutput")
            P.barrier()
            for b in range(8):
                P.dma(dm[:, b * 512:(b + 1) * 512], g.hT_blk[b][:, :])
            P.barrier()
        if on("ffn1"):
            with nc.named_scope("L%d_ffn1" % layer):
                phase_ffn1(P, g, layer, g.ncols[:, 2, :])
        if on("ffn2"):
            with nc.named_scope("L%d_ffn2" % layer):
                phase_ffn2(P, g, layer, g.ncols[:, 3, :])
        if on("ple"):
            with nc.named_scope("L%d_ple" % layer):
                phase_ple(P, g, layer)
        if st is not None and "dumph" in st:
            dh = P.dram("dbg_h%d" % layer, [D, S], F32, "ExternalOutput")
            P.barrier()
            for b in range(8):
                P.dma(dh[:, b * 512:(b + 1) * 512], g.hT_blk[b][:, :])
            P.barrier()
    if on("out"):
        with nc.named_scope("out_tr"):
            phase_out_transpose(P, g)
    P.finish()
    return nc, P


_CACHE = {}


def make_in_maps(inputs, ncores=NCORES):
    consts = host_consts()
    xs = [inputs["x_prompt"][0], inputs["x_prompt"][1]] + [inputs["x_sample"][i] for i in range(4)]
    ps = [inputs["p_prompt"][:, 0], inputs["p_prompt"][:, 1]] + [inputs["p_sample"][:, i] for i in range(4)]
    maps = []
    rpbp = rpb_pad(np.asarray(inputs["ab_rpb"], dtype=np.float32))
    for c in range(ncores):
        s = c if c < 6 else c - 6
        m = {"x": np.ascontiguousarray(xs[s], dtype=np.float32), "p": np.ascontiguousarray(ps[s], dtype=np.float32)}
        for name, _ in WEIGHT_SPECS:
            m[name] = np.ascontiguousarray(inputs[name], dtype=np.float32)
        m.update(consts)
        m["rpbpad"] = rpbp
        maps.append(m)
    return maps


def kernel(**inputs):
    if "nc" not in _CACHE:
        _CACHE["nc"] = build_program()[0]
    nc = _CACHE["nc"]
    maps = make_in_maps(inputs)
    res = run_bass_kernel_spmd(nc, maps, core_ids=list(range(NCORES)))
    ys = [np.asarray(res.results[c]["y"], dtype=np.float32) for c in range(6)]
    y_prompt = np.stack(ys[0:2], 0)
    y_sample = np.stack(ys[2:6], 0)
    return (y_prompt, y_sample)
```
